# Optimizing a Trainium2 kernel written in Bass

```python
import math
import jax, jax.numpy as jnp
from jax import lax
import numpy as np

D_MODEL = 1024
BATCH = 8
SEQ = 4096
DEPTH = 2

N_HEADS_MLA = 8
MLA_QK_NOPE = 64
MLA_QK_ROPE = 32
MLA_V = 64
MLA_Q_RANK = 768
MLA_KV_RANK = 256
MLA_QK = MLA_QK_NOPE + MLA_QK_ROPE

N_HEADS_SB = 8
SB_HEAD = 64

N_HEADS_MOBA = 8
MOBA_HEAD = 64
MOBA_BLOCK = 256
MOBA_TOPK = 3
MOBA_Q_CHUNK = 32

Q_BLOCK = 128
ROPE_THETA = 500000.0
PARTIAL_ROPE_DIM = MOBA_HEAD // 4
D_FF = 4 * D_MODEL
N_BRANCH = 3
EPS = 1e-6
NEG_INF = -1e30

W_MLA = N_HEADS_MLA * MLA_V
W_SB = N_HEADS_SB * SB_HEAD
W_MOBA = N_HEADS_MOBA * MOBA_HEAD

IN_SPLITS = (MLA_Q_RANK, MLA_KV_RANK, MLA_QK_ROPE, 3 * W_SB, 3 * W_MOBA, N_BRANCH * D_MODEL)
D_IN = sum(IN_SPLITS)
SPLIT_POINTS = [int(v) for v in np.cumsum(IN_SPLITS)[:-1]]

kernel_name = "hybrid_mla_stickbreak_moba_adaln"


def rmsnorm(x, g):
    xf = x.astype(jnp.float32)
    y = xf * lax.rsqrt(jnp.mean(xf * xf, axis=-1, keepdims=True) + EPS)
    return y.astype(x.dtype) * g


def modulate(h, shift, scale):
    return h * (1.0 + scale[:, None, :]) + shift[:, None, :]


def rope_tables(positions, rot_dim, dtype):
    inv = ROPE_THETA ** (-jnp.arange(0, rot_dim, 2, dtype=jnp.float32) / rot_dim)
    ang = positions.astype(jnp.float32)[..., None] * inv
    return jnp.cos(ang)[:, None].astype(dtype), jnp.sin(ang)[:, None].astype(dtype)


def apply_rope(x, cos, sin):
    rd = 2 * cos.shape[-1]
    xr, xp = x[..., :rd], x[..., rd:]
    x1, x2 = xr[..., : rd // 2], xr[..., rd // 2:]
    rot = jnp.concatenate([x1 * cos - x2 * sin, x1 * sin + x2 * cos], axis=-1)
    return jnp.concatenate([rot, xp], axis=-1)


def to_heads(t, n_heads):
    b, s, w = t.shape
    return t.reshape(b, s, n_heads, w // n_heads).transpose(0, 2, 1, 3)


def from_heads(t):
    b, h, s, d = t.shape
    return t.transpose(0, 2, 1, 3).reshape(b, s, h * d)


def causal_softmax_attention(q, k, v, scale):
    b, h, s, dq = q.shape
    nq = s // Q_BLOCK
    qb = jnp.moveaxis(q.reshape(b, h, nq, Q_BLOCK, dq), 2, 0)
    kpos = jnp.arange(s)

    def body(args):
        qi, i = args
        sc = jnp.einsum('bhqd,bhkd->bhqk', qi, k).astype(jnp.float32) * scale
        qpos = i * Q_BLOCK + jnp.arange(Q_BLOCK)
        sc = jnp.where(kpos[None, :] <= qpos[:, None], sc, NEG_INF)
        p = jax.nn.softmax(sc, axis=-1).astype(v.dtype)
        return jnp.einsum('bhqk,bhkd->bhqd', p, v)

    o = lax.map(body, (qb, jnp.arange(nq)))
    return jnp.moveaxis(o, 0, 2).reshape(b, h, s, v.shape[-1])


def stick_breaking_attention(q, k, v):
    b, h, s, d = q.shape
    scale = 1.0 / math.sqrt(d)
    nq = s // Q_BLOCK
    qb = jnp.moveaxis(q.reshape(b, h, nq, Q_BLOCK, d), 2, 0)
    kpos = jnp.arange(s)

    def body(args):
        qi, i = args
        z = jnp.einsum('bhqd,bhkd->bhqk', qi, k).astype(jnp.float32) * scale
        qpos = i * Q_BLOCK + jnp.arange(Q_BLOCK)
        past = kpos[None, :] < qpos[:, None]
        log_beta = jax.nn.log_sigmoid(z)
        log_keep = jnp.where(past, jax.nn.log_sigmoid(-z), 0.0)
        later = lax.cumsum(log_keep, axis=3, reverse=True) - log_keep
        a = jnp.where(past, jnp.exp(log_beta + later), 0.0).astype(v.dtype)
        return jnp.einsum('bhqk,bhkd->bhqd', a, v)

    o = lax.map(body, (qb, jnp.arange(nq)))
    return jnp.moveaxis(o, 0, 2).reshape(b, h, s, d)


def moba_attention(q, k, v):
    b, h, s, d = q.shape
    scale = 1.0 / math.sqrt(d)
    nb = -(-s // MOBA_BLOCK)
    pad = nb * MOBA_BLOCK - s
    kb = jnp.pad(k, ((0, 0), (0, 0), (0, pad), (0, 0))).reshape(b, h, nb, MOBA_BLOCK, d)
    vb = jnp.pad(v, ((0, 0), (0, 0), (0, pad), (0, 0))).reshape(b, h, nb, MOBA_BLOCK, d)
    kmean = jnp.mean(kb.astype(jnp.float32), axis=3)

    cur = jnp.arange(s) // MOBA_BLOCK
    gate = jnp.einsum('bhsd,bhnd->bhsn', q.astype(jnp.float32), kmean)
    past_blk = jnp.arange(nb)[None, :] < cur[:, None]
    gate = jnp.where(past_blk, gate, NEG_INF)
    topk = min(MOBA_TOPK, nb)
    _, idx = lax.top_k(gate, topk)
    valid = jnp.arange(topk)[None, :] < cur[:, None]

    nc = s // MOBA_Q_CHUNK
    qc = jnp.moveaxis(q.reshape(b, h, nc, MOBA_Q_CHUNK, d), 2, 0)
    ic = jnp.moveaxis(idx.reshape(b, h, nc, MOBA_Q_CHUNK, topk), 2, 0)
    vc = valid.reshape(nc, MOBA_Q_CHUNK, topk)
    gather = jax.vmap(jax.vmap(lambda blocks, ids: blocks[ids]))

    def body(args):
        qi, ii, vi, ci = args
        kg = gather(kb, ii)
        vg = gather(vb, ii)
        qpos = ci * MOBA_Q_CHUNK + jnp.arange(MOBA_Q_CHUNK)
        s_sel = jnp.einsum('bhqd,bhqnjd->bhqnj', qi, kg).astype(jnp.float32) * scale
        s_sel = jnp.where(vi[:, :, None], s_sel, NEG_INF).reshape(b, h, MOBA_Q_CHUNK, topk * MOBA_BLOCK)
        ob = (ci * MOBA_Q_CHUNK) // MOBA_BLOCK
        k_own = lax.dynamic_index_in_dim(kb, ob, axis=2, keepdims=False)
        v_own = lax.dynamic_index_in_dim(vb, ob, axis=2, keepdims=False)
        s_own = jnp.einsum('bhqd,bhjd->bhqj', qi, k_own).astype(jnp.float32) * scale
        kpos = ob * MOBA_BLOCK + jnp.arange(MOBA_BLOCK)
        s_own = jnp.where(kpos[None, :] <= qpos[:, None], s_own, NEG_INF)
        p = jax.nn.softmax(jnp.concatenate([s_sel, s_own], axis=-1), axis=-1).astype(v.dtype)
        p_sel = p[..., : topk * MOBA_BLOCK].reshape(b, h, MOBA_Q_CHUNK, topk, MOBA_BLOCK)
        p_own = p[..., topk * MOBA_BLOCK:]
        return (jnp.einsum('bhqnj,bhqnjd->bhqd', p_sel, vg)
                + jnp.einsum('bhqj,bhjd->bhqd', p_own, v_own))

    o = lax.map(body, (qc, ic, vc, jnp.arange(nc)))
    return jnp.moveaxis(o, 0, 2).reshape(b, h, s, d)


def setup_inputs(seed: int = 0) -> dict:
    key = jax.random.key(seed)
    ks = jax.random.split(key, 24)
    f32 = jnp.float32

    def nrm(k, shape, fan_in):
        return jax.random.normal(k, shape, f32) * (fan_in ** -0.5)

    def gain(k, shape):
        return 1.0 + 0.02 * jax.random.normal(k, shape, f32)

    L = DEPTH
    return {
        "x": jax.random.normal(ks[0], (BATCH, SEQ, D_MODEL), f32),
        "c": jax.random.normal(ks[1], (BATCH, D_MODEL), f32),
        "positions": jnp.broadcast_to(jnp.arange(SEQ, dtype=jnp.int32)[None, :], (BATCH, SEQ)),
        "w_ada": nrm(ks[2], (L, D_MODEL, 6 * D_MODEL), D_MODEL),
        "b_ada": 0.02 * jax.random.normal(ks[3], (L, 6 * D_MODEL), f32),
        "norm1_g": gain(ks[4], (L, D_MODEL)),
        "norm2_g": gain(ks[5], (L, D_MODEL)),
        "w_in": nrm(ks[6], (L, D_MODEL, D_IN), D_MODEL),
        "q_norm_g": gain(ks[7], (L, MLA_Q_RANK)),
        "w_uq": nrm(ks[8], (L, MLA_Q_RANK, N_HEADS_MLA * MLA_QK), MLA_Q_RANK),
        "kv_norm_g": gain(ks[9], (L, MLA_KV_RANK)),
        "w_ukv": nrm(ks[10], (L, MLA_KV_RANK, N_HEADS_MLA * (MLA_QK_NOPE + MLA_V)), MLA_KV_RANK),
        "w_o_mla": nrm(ks[11], (L, W_MLA, D_MODEL), W_MLA),
        "w_o_sb": nrm(ks[12], (L, W_SB, D_MODEL), W_SB),
        "w_o_moba": nrm(ks[13], (L, W_MOBA, D_MODEL), W_MOBA),
        "w_out": nrm(ks[14], (L, D_MODEL, D_MODEL), D_MODEL),
        "w_ff1": nrm(ks[15], (L, D_MODEL, D_FF), D_MODEL),
        "w_ff2": nrm(ks[16], (L, D_FF, D_MODEL), D_FF),
        "final_norm_g": gain(ks[17], (D_MODEL,)),
    }


def reference(x, c, positions, w_ada, b_ada, norm1_g, norm2_g, w_in, q_norm_g, w_uq,
              kv_norm_g, w_ukv, w_o_mla, w_o_sb, w_o_moba, w_out, w_ff1, w_ff2, final_norm_g):
    b, s, _ = x.shape
    cos_mla, sin_mla = rope_tables(positions, MLA_QK_ROPE, x.dtype)
    cos_mb, sin_mb = rope_tables(positions, PARTIAL_ROPE_DIM, x.dtype)
    c_act = jax.nn.silu(c)
    mla_scale = 1.0 / math.sqrt(MLA_QK)

    for l in range(DEPTH):
        mod = c_act @ w_ada[l] + b_ada[l]
        shift1, scale1, gate1, shift2, scale2, gate2 = jnp.split(mod, 6, axis=-1)

        hdn = modulate(rmsnorm(x, norm1_g[l]), shift1, scale1)
        proj = hdn @ w_in[l]
        q_lat, c_kv, k_pe, sb_qkv, mb_qkv, gates = jnp.split(proj, SPLIT_POINTS, axis=-1)

        q = to_heads(rmsnorm(q_lat, q_norm_g[l]) @ w_uq[l], N_HEADS_MLA)
        q_nope, q_pe = q[..., :MLA_QK_NOPE], q[..., MLA_QK_NOPE:]
        kv = to_heads(rmsnorm(c_kv, kv_norm_g[l]) @ w_ukv[l], N_HEADS_MLA)
        k_nope, v_mla = kv[..., :MLA_QK_NOPE], kv[..., MLA_QK_NOPE:]
        q_pe = apply_rope(q_pe, cos_mla, sin_mla)
        k_pe = apply_rope(k_pe[:, None], cos_mla, sin_mla)
        q_mla = jnp.concatenate([q_nope, q_pe], axis=-1)
        k_mla = jnp.concatenate([k_nope, jnp.broadcast_to(k_pe, k_nope.shape[:3] + (MLA_QK_ROPE,))], axis=-1)
        o_mla = from_heads(causal_softmax_attention(q_mla, k_mla, v_mla, mla_scale))

        q_sb, k_sb, v_sb = [to_heads(t, N_HEADS_SB) for t in jnp.split(sb_qkv, 3, axis=-1)]
        o_sb = from_heads(stick_breaking_attention(q_sb, k_sb, v_sb))

        q_mb, k_mb, v_mb = [to_heads(t, N_HEADS_MOBA) for t in jnp.split(mb_qkv, 3, axis=-1)]
        q_mb = apply_rope(q_mb, cos_mb, sin_mb)
        k_mb = apply_rope(k_mb, cos_mb, sin_mb)
        o_mb = from_heads(moba_attention(q_mb, k_mb, v_mb))

        g_a, g_b, g_c = jnp.split(jax.nn.sigmoid(gates), N_BRANCH, axis=-1)
        merged = (g_a * (o_mla @ w_o_mla[l]) + g_b * (o_sb @ w_o_sb[l])
                  + g_c * (o_mb @ w_o_moba[l]))
        x = x + gate1[:, None, :] * (merged @ w_out[l])

        hdn = modulate(rmsnorm(x, norm2_g[l]), shift2, scale2)
        ff = jnp.square(jax.nn.relu(hdn @ w_ff1[l])) @ w_ff2[l]
        x = x + gate2[:, None, :] * ff

    return rmsnorm(x, final_norm_g)
```

```python
import math
from contextlib import ExitStack

import numpy as np
import concourse.bass as bass
import concourse.mybir as mybir
from concourse.bass_utils import run_bass_kernel_spmd

F32 = mybir.dt.float32
BF16 = mybir.dt.bfloat16
I32 = mybir.dt.int32
AF = mybir.ActivationFunctionType
ALU = mybir.AluOpType
AX = mybir.AxisListType

D = 1024
S = 4096
NT = S // 128
NG = S // 512
DEPTH = 2
DIN = 7200
DFF = 4096
EPS = 1e-6
THETA = 500000.0
OFF_QLAT, OFF_CKV, OFF_KPE = 0, 768, 1024
OFF_SBQ, OFF_SBK, OFF_SBV = 1056, 1568, 2080
OFF_MBQ, OFF_MBK, OFF_MBV = 2592, 3104, 3616
OFF_G = 4128
NPROJ = 4128
MASKNEG = -30000.0

N_DMA_SEMS = 24
SAME_ENGINE_SYNC = True


class Op:
    __slots__ = ("idx", "eng", "fn", "dma", "deps", "sem", "val", "inc", "waits", "known")

    def __init__(self, idx, eng, fn, dma):
        self.idx = idx; self.eng = eng; self.fn = fn; self.dma = dma
        self.deps = (); self.sem = None; self.val = 0; self.inc = False
        self.waits = (); self.known = None


class Prog:
    ENGS = ("sp", "act", "dve", "pool", "pe")

    def __init__(self, nc):
        self.nc = nc
        self.ops = []
        self.last_w = {}
        self.readers = {}
        self.dma_rr = {e: 0 for e in self.ENGS}
        self.dma_last = {}
        self.last_eng = {}

    def barrier(self):
        deps = list(self.last_eng.values()) + list(self.dma_last.values())
        for e in self.ENGS:
            op = self.add(e, None)
            dd = {d.idx: d for d in op.deps}
            for d in deps:
                dd[d.idx] = d
            op.deps = list(dd.values())

    def add(self, eng, fn, reads=(), writes=(), dma=False):
        op = Op(len(self.ops), eng, fn, dma)
        excl = [k for k in reads if isinstance(k, str) and k.startswith("bk")]
        if excl:
            writes = list(writes) + [k for k in excl if k not in writes]
            reads = [k for k in reads if k not in excl]
        deps = {}
        for k in reads:
            w = self.last_w.get(k)
            if w is not None:
                deps[w.idx] = w
        for k in writes:
            w = self.last_w.get(k)
            if w is not None:
                deps[w.idx] = w
            for r in self.readers.get(k, ()):
                deps[r.idx] = r
        for k in reads:
            self.readers.setdefault(k, []).append(op)
        for k in writes:
            self.last_w[k] = op
            self.readers[k] = []
        if dma:
            slot = (eng, self.dma_rr[eng] % N_DMA_SEMS)
            self.dma_rr[eng] += 1
            prev = self.dma_last.get(slot)
            if prev is not None:
                deps[prev.idx] = prev
            self.dma_last[slot] = op
            op.sem = slot
            op.val = (prev.val if prev is not None else 0) + 16
        op.deps = list(deps.values())
        self.ops.append(op)
        if fn is not None and not dma:
            self.last_eng[eng] = op
        return op

    def finalize_and_emit(self, sems):
        for op in self.ops:
            for d in op.deps:
                if d.dma:
                    continue
                if d.eng != op.eng or op.dma or (SAME_ENGINE_SYNC and d.eng != "pe"):
                    d.inc = True
        cnt = {e: 0 for e in self.ENGS}
        for op in self.ops:
            if not op.dma and op.inc:
                cnt[op.eng] += 1
                op.sem = op.eng
                op.val = cnt[op.eng]
        clock = {e: {} for e in self.ENGS}
        for op in self.ops:
            ck = clock[op.eng]
            waits = {}
            changed = False
            for d in sorted(op.deps, key=lambda o: -o.idx):
                if d.sem is None:
                    continue
                if (not d.dma) and d.eng == op.eng and not op.dma and not (SAME_ENGINE_SYNC and d.eng != "pe"):
                    continue
                if ck.get(d.sem, 0) >= d.val:
                    continue
                if waits.get(d.sem, 0) < d.val:
                    waits[d.sem] = d.val
                if not changed:
                    ck = dict(ck); changed = True
                for s, v in d.known.items():
                    if ck.get(s, 0) < v:
                        ck[s] = v
                if ck.get(d.sem, 0) < d.val:
                    ck[d.sem] = d.val
            op.waits = list(waits.items())
            if changed:
                clock[op.eng] = ck
            if op.sem is not None:
                if not op.dma:
                    ck2 = dict(ck); ck2[op.sem] = op.val
                    op.known = ck2
                    if op.eng == "pe" or not SAME_ENGINE_SYNC:
                        clock[op.eng] = ck2
                else:
                    op.known = ck
        per_eng = {e: [o for o in self.ops if o.eng == e] for e in self.ENGS}
        self.stats = {e: (len(per_eng[e]), sum(len(o.waits) for o in per_eng[e])) for e in self.ENGS}

        def emit(e, name):
            for op in per_eng[name]:
                for s, v in op.waits:
                    e.wait_ge(sems[s], v)
                if op.fn is not None:
                    ins = op.fn(e)
                    if op.sem is not None:
                        ins.then_inc(sems[op.sem], 16 if op.dma else 1)

        with self.nc.Block() as block:
            @block.sync
            def _(e):
                emit(e, "sp")

            @block.scalar
            def _(e):
                emit(e, "act")

            @block.vector
            def _(e):
                emit(e, "dve")

            @block.gpsimd
            def _(e):
                emit(e, "pool")

            @block.tensor
            def _(e):
                emit(e, "pe")


class Arena:
    def __init__(self, ap, words):
        self.ap = ap; self.words = words; self.off = 0; self.peak = 0

    def mark(self):
        return self.off

    def release(self, m):
        self.off = m

    def alloc(self, shape, dtype):
        p = shape[0]
        n = 1
        for s in shape[1:]:
            n *= s
        esz = 4 if dtype in (F32, I32) else 2
        words = (n * esz + 3) // 4
        words = (words + 7) // 8 * 8
        assert self.off + words <= self.words, ("arena overflow", self.off, words, self.words)
        sl = self.ap[:, self.off:self.off + words]
        self.off += words
        self.peak = max(self.peak, self.off)
        if dtype != F32:
            sl = sl.bitcast(dtype)
        sl = sl[0:p, 0:n]
        if len(shape) == 3:
            sl = sl.rearrange("p (a b) -> p a b", a=shape[1])
        elif len(shape) == 4:
            sl = sl.rearrange("p (a b c) -> p a b c", a=shape[1], b=shape[2])
        return sl


class Builder:
    def __init__(self, debug=None):
        self.debug = debug or {}
        self.nc = bass.Bass("TRN2", target_bir_lowering=False)
        self.P = Prog(self.nc)

    def dma(self, eng, out, in_, reads, writes, **kw):
        return self.P.add(eng, lambda e: e.dma_start(out=out, in_=in_, **kw), reads, writes, dma=True)

    def mm(self, out, lhsT, rhs, start, stop, reads, writes):
        return self.P.add("pe", lambda e: e.matmul(out, lhsT=lhsT, rhs=rhs, start=start, stop=stop), reads, writes)

    def pe_fence(self, reads, writes):
        z = self.zeros_b
        d = self.dummy_ps
        return self.P.add("pe", lambda e: e.matmul(d, lhsT=z[:, 0:1], rhs=z[:, 1:2], start=True, stop=True),
                          list(reads), list(writes) + ["bk3"])

    def tr(self, out, in_, ident, reads, writes):
        return self.P.add("pe", lambda e: e.transpose(out=out, in_=in_, identity=ident), reads, writes)

    def act(self, out, in_, func, reads, writes, bias=None, scale=None, accum_out=None):
        kw = {}
        if bias is not None:
            kw["bias"] = bias
        if scale is not None:
            kw["scale"] = scale
        if accum_out is not None:
            kw["accum_out"] = accum_out
        return self.P.add("act", lambda e: e.activation(out=out, in_=in_, func=func, **kw), reads, writes)

    def copy(self, eng, out, in_, reads, writes):
        if eng == "act":
            return self.act(out, in_, AF.Copy, reads, writes)
        return self.P.add(eng, lambda e: e.tensor_copy(out=out, in_=in_), reads, writes)

    def tt(self, eng, out, in0, in1, op, reads, writes):
        return self.P.add(eng, lambda e: e.tensor_tensor(out=out, in0=in0, in1=in1, op=op), reads, writes)

    def ts(self, eng, out, in0, s1, s2, op0, op1, reads, writes):
        if s2 is None:
            return self.P.add(eng, lambda e: e.tensor_scalar(out=out, in0=in0, scalar1=s1, scalar2=None, op0=op0), reads, writes)
        return self.P.add(eng, lambda e: e.tensor_scalar(out=out, in0=in0, scalar1=s1, scalar2=s2, op0=op0, op1=op1), reads, writes)

    def recip(self, out, in_, reads, writes):
        return self.P.add("dve", lambda e: e.reciprocal(out=out, in_=in_), reads, writes)

    def memset(self, eng, ap, val, reads, writes):
        return self.P.add(eng, lambda e: e.memset(ap, val), reads, writes)

    def asel(self, out, in_, pattern, cmp, fill, base, cm, reads, writes):
        return self.P.add("pool", lambda e: e.affine_select(out=out, in_=in_, pattern=pattern, compare_op=cmp,
                                                            fill=fill, base=base, channel_multiplier=cm), reads, writes)

    def vmax(self, out, in_, reads, writes):
        return self.P.add("dve", lambda e: e.max(out=out, in_=in_), reads, writes)

    def build(self):
        nc = self.nc
        dbg = self.debug
        stop_after = dbg.get("stop_after", None)
        n_layers = dbg.get("n_layers", DEPTH)

        self.declared = []
        SHAPES = {"x": ([S, D], F32), "c": ([D], F32), "pos": ([S], I32), "w_ada": ([DEPTH, D, 6 * D], F32),
                  "b_ada": ([DEPTH, 6 * D], F32), "norm1_g": ([DEPTH, D], F32), "norm2_g": ([DEPTH, D], F32),
                  "w_in": ([DEPTH, D, DIN], F32), "q_norm_g": ([DEPTH, 768], F32), "w_uq": ([DEPTH, 768, 768], F32),
                  "kv_norm_g": ([DEPTH, 256], F32), "w_ukv": ([DEPTH, 256, 1024], F32), "w_o_mla": ([DEPTH, 512, D], F32),
                  "w_o_sb": ([DEPTH, 512, D], F32), "w_o_moba": ([DEPTH, 512, D], F32), "w_out": ([DEPTH, D, D], F32),
                  "w_ff1": ([DEPTH, D, DFF], F32), "w_ff2": ([DEPTH, DFF, D], F32), "final_norm_g": ([D], F32)}
        bld = self

        class LazyIn(dict):
            def __missing__(self, name):
                shp, dt = SHAPES[name]
                ap = nc.dram_tensor(name, shp, dt, kind="ExternalInput").ap()
                bld.declared.append(name)
                self[name] = ap
                return ap

        IN = LazyIn()
        self.IN = IN
        x_in = IN["x"]; c_in = IN["c"]; pos_in = IN["pos"]
        w_ada = IN["w_ada"]; b_ada = IN["b_ada"]
        norm1_g = IN["norm1_g"]; norm2_g = IN["norm2_g"]; q_norm_g = IN["q_norm_g"]; kv_norm_g = IN["kv_norm_g"]
        fin_g = IN["final_norm_g"]
        y_out = nc.dram_tensor("y", [S, D], F32, kind="ExternalOutput").ap()

        dbg_scr = dbg.get("scratch_out", ())

        def scr(name, shape, dt=BF16):
            kind = "ExternalOutput" if name in dbg_scr else "Internal"
            return nc.dram_tensor(name, shape, dt, kind=kind).ap()

        modd = scr("modd", [DEPTH, 6 * D], F32)
        HT = scr("HT", [D, S])
        QA = scr("QA", [768, S]); KA = scr("KA", [512, S]); KPE = scr("KPE", [32, S]); VA = scr("VA", [S, 512])
        QS = scr("QS", [512, S]); KS = scr("KS", [512, S]); VS = scr("VS", [S, 512])
        QM = scr("QM", [512, S]); KM = scr("KM", [512, S]); VM = scr("VM", [S, 512]); MB = scr("MB", [128, S])
        OH = scr("OH", [16, S])
        OT = scr("OT", [3, 512, S])
        XM = scr("XM", [S, D], F32)
        XL = scr("XL", [S, D], F32)

        with ExitStack() as st:
            sems = {}
            for e in Prog.ENGS:
                sems[e] = st.enter_context(nc.semaphore("s_" + e))
            for e in ("sp", "pool", "act"):
                for i in range(N_DMA_SEMS):
                    sems[(e, i)] = st.enter_context(nc.semaphore("d_%s_%d" % (e, i)))
            AW = 50 * 1024
            arena_t = st.enter_context(nc.sbuf_tensor("arena", [128, AW], F32))
            A = Arena(arena_t[:], AW)
            banks = [st.enter_context(nc.psum_tensor("bank%d" % i, [128, 512], F32)) for i in range(8)]
            BK = [b[:] for b in banks]
            BKH = [b[:].bitcast(BF16) for b in banks]

            P = self.P
            B = self

            ident_b = A.alloc([128, 128], BF16)
            ident_f = A.alloc([128, 128], F32)
            ones_f = A.alloc([128, 1], F32)
            tri_neg = A.alloc([128, 128], BF16)
            neg_ones = A.alloc([128, 128], BF16)
            zeros_b = A.alloc([128, 64], BF16)
            cosA = A.alloc([128, NT, 16], F32); sinA = A.alloc([128, NT, 16], F32)
            cosB = A.alloc([128, NT, 8], F32); sinB = A.alloc([128, NT, 8], F32)
            modT = A.alloc([128, DEPTH, 48], F32)
            n1g = A.alloc([128, DEPTH, 8], F32); n2g = A.alloc([128, DEPTH, 8], F32)
            qng = A.alloc([128, DEPTH, 6], F32); kvng = A.alloc([128, DEPTH, 2], F32)
            A1 = A.alloc([128, DEPTH, 8], F32); A2 = A.alloc([128, DEPTH, 8], F32)
            gbc = A.alloc([128, D], F32)
            fing_bc = A.alloc([128, D], F32)
            persist_mark = A.mark()
            self.zeros_b = zeros_b
            self.dummy_ps = BK[3][0:1, 0:1]

            B.memset("pool", ident_f, 1.0, [], ["ident_f"])
            B.asel(ident_f, ident_f, [[-1, 128]], ALU.is_equal, 0.0, 0, 1, ["ident_f"], ["ident_f"])
            B.copy("dve", ident_b, ident_f, ["ident_f"], ["ident_b"])
            B.memset("pool", ones_f, 1.0, [], ["ones_f"])
            B.memset("pool", neg_ones, -1.0, [], ["neg_ones"])
            B.memset("pool", tri_neg, -1.0, [], ["tri_neg"])
            B.asel(tri_neg, tri_neg, [[-1, 128]], ALU.is_ge, 0.0, 0, 1, ["tri_neg"], ["tri_neg"])
            B.memset("pool", zeros_b, 0.0, [], ["zeros_b"])

            m0 = A.mark()
            pos_i = A.alloc([128, NT], I32); pos_f = A.alloc([128, NT], F32)
            invA = A.alloc([128, 16], F32); invB = A.alloc([128, 8], F32)
            B.dma("sp", pos_i, pos_in.rearrange("(j p) -> p j", p=128), [], ["pos_i"], allow_slow_non_contiguous=True)
            B.copy("dve", pos_f, pos_i, ["pos_i"], ["pos_f"])
            for i in range(16):
                B.memset("pool", invA[:, i:i + 1], float(np.float32(THETA) ** np.float32(-(2.0 * i) / 32.0)) / (2 * math.pi), [], ["invA"])
            for i in range(8):
                B.memset("pool", invB[:, i:i + 1], float(np.float32(THETA) ** np.float32(-(2.0 * i) / 16.0)) / (2 * math.pi), [], ["invB"])

            def trig_table(dst, inv, n, shift, key):
                y = A.alloc([128, NT, n], F32); yi = A.alloc([128, NT, n], I32); yf = A.alloc([128, NT, n], F32)
                w = A.alloc([128, NT, n], F32)
                for j in range(NT):
                    B.ts("dve", y[:, j, :], inv, pos_f[:, j:j + 1], shift, ALU.mult, ALU.add, ["pos_f", "invA", "invB"], [key + "y"])
                B.copy("dve", yi, y, [key + "y"], [key + "yi"])
                B.copy("dve", yf, yi, [key + "yi"], [key + "yf"])
                B.tt("dve", y, y, yf, ALU.subtract, [key + "y", key + "yf"], [key + "y"])
                B.ts("dve", w, y, 0.5, None, ALU.is_gt, None, [key + "y"], [key + "w"])
                B.tt("dve", y, y, w, ALU.subtract, [key + "y", key + "w"], [key + "y"])
                B.ts("dve", w, y, -0.5, None, ALU.is_lt, None, [key + "y"], [key + "w"])
                B.tt("dve", y, y, w, ALU.add, [key + "y", key + "w"], [key + "y"])
                B.act(dst, y, AF.Sin, [key + "y"], [key], scale=2 * math.pi)

            for (dst, inv, n, shift, key) in ((sinA, invA, 16, 0.0, "sinA"), (cosA, invA, 16, 0.25, "cosA"),
                                               (sinB, invB, 8, 0.0, "sinB"), (cosB, invB, 8, 0.25, "cosB")):
                trig_table(dst, inv, n, shift, key)

            oh = A.alloc([16, S], BF16)
            B.memset("pool", oh, 1.0, [], ["oh"])
            B.asel(oh, oh, [[1, S]], ALU.is_ge, 0.0, 0, -256, ["oh"], ["oh"])
            B.asel(oh, oh, [[-1, S]], ALU.is_gt, 0.0, 256, 256, ["oh"], ["oh"])
            B.dma("sp", OH, oh, ["oh"], ["OH"])

            if not dbg.get("no_mod", False):
                cT = A.alloc([128, 8], F32); cact = A.alloc([128, 8], F32)
                B.dma("sp", cT, c_in.rearrange("(j p) -> p j", p=128), [], ["cT"], allow_slow_non_contiguous=True)
                B.act(cact, cT, AF.Silu, ["cT"], ["cact"])
                modrow = A.alloc([1, 6 * D], F32); brow = A.alloc([1, 6 * D], F32)
                wa = [A.alloc([128, 8, 512], F32), A.alloc([128, 8, 512], F32)]
                it = 0
                for l in range(DEPTH):
                    B.dma("sp", brow, b_ada[l:l + 1, :], [], ["brow"])
                    for n in range(12):
                        wt = wa[it % 2]; wk = "wa%d" % (it % 2)
                        B.dma("sp" if it % 2 == 0 else "pool", wt, w_ada[l, :, n * 512:(n + 1) * 512].rearrange("(j p) c -> p j c", p=128), [], [wk])
                        bk = 1 + (it % 2)
                        B.pe_fence(["cact", wk, "zeros_b"], ["bk%d" % bk])
                        for j in range(8):
                            B.mm(BK[bk][0:1, :], cact[:, j:j + 1], wt[:, j, :], j == 0, j == 7, [], [])
                        B.pe_fence(["cact", wk, "zeros_b"], ["bk%d" % bk])
                        B.tt("dve", modrow[:, n * 512:(n + 1) * 512], BK[bk][0:1, :], brow[:, n * 512:(n + 1) * 512], ALU.add,
                             ["bk%d" % bk, "brow"], ["modrow"])
                        it += 1
                    B.dma("sp", modd[l:l + 1, :], modrow, ["modrow"], [("modd", l)])
                    B.dma("sp", modT[:, l, :], modd[l].rearrange("(c p) -> p c", p=128), [("modd", l)], [("modT", l)], allow_slow_non_contiguous=True)
                for (dst, src, nch, key) in ((n1g, norm1_g, 8, "n1g"), (n2g, norm2_g, 8, "n2g"), (qng, q_norm_g, 6, "qng"), (kvng, kv_norm_g, 2, "kvng")):
                    for l in range(DEPTH):
                        B.dma("sp", dst[:, l, :], src[l].rearrange("(c p) -> p c", p=128), [], [key], allow_slow_non_contiguous=True)
                B.dma("sp", fing_bc, fin_g.partition_broadcast(128), [], ["fing"])
                for l in range(DEPTH):
                    B.ts("dve", A1[:, l, :], modT[:, l, 8:16], 1.0, None, ALU.add, None, [("modT", l)], ["A1"])
                    B.tt("dve", A1[:, l, :], A1[:, l, :], n1g[:, l, :], ALU.mult, ["A1", "n1g"], ["A1"])
                    B.ts("dve", A2[:, l, :], modT[:, l, 32:40], 1.0, None, ALU.add, None, [("modT", l)], ["A2"])
                    B.tt("dve", A2[:, l, :], A2[:, l, :], n2g[:, l, :], ALU.mult, ["A2", "n2g"], ["A2"])
            A.release(m0)
            SETUP_KEYS = ["A1", "A2", "sinA", "cosA", "sinB", "cosB", "OH", "ident_b", "ident_f", "tri_neg", "fing",
                          "qng", "kvng", "ones_f", "neg_ones", "zeros_b", "wa0", "wa1", "modrow", "brow", "cact", "oh"]
            self.barrier(SETUP_KEYS + [("modT", l) for l in range(DEPTH)], "setup")

            if stop_after == "setup":
                self.finish(sems, [])
                return nc

            out_keys = []
            for l in range(n_layers):
                x_src = x_in if l == 0 else XL
                phases = dbg.get("phases", (1, 2, 3, 4))
                if 1 in phases:
                    self.phase1(l, A, BK, BKH, locals())
                if stop_after == ("p1", l):
                    break
                if 2 in phases:
                    self.phase2(l, A, BK, BKH, locals())
                if stop_after == ("p2", l):
                    break
                if 3 in phases:
                    self.phase3(l, A, BK, BKH, locals())
                if stop_after == ("p3", l):
                    break
                if 4 in phases:
                    self.phase4(l, A, BK, BKH, locals(), last=(l == DEPTH - 1))
                if stop_after == ("p4", l):
                    break
            self.finish(sems, [])
            self.arena_peak = A.peak
        return nc

    def barrier(self, keys=None, name=None):
        self.P.barrier()
        self.fence = []

    def finish(self, sems, keys):
        P = self.P
        last_dma_keys = []
        for slot, op in P.dma_last.items():
            k = ("lastdma", slot)
            P.last_w[k] = op
            last_dma_keys.append(k)
        P.add("sp", None, last_dma_keys, [])
        P.finalize_and_emit(sems)

    def phase1(self, l, A, BK, BKH, env):
        B = self; P = self.P
        g = env
        w_in = self.IN["w_in"]; w_uq = self.IN["w_uq"]; w_ukv = self.IN["w_ukv"]
        ident_b = g["ident_b"]; ident_f = g["ident_f"]; ones_f = g["ones_f"]
        cosA, sinA, cosB, sinB = g["cosA"], g["sinA"], g["cosB"], g["sinB"]
        modT, A1, qng, kvng = g["modT"], g["A1"], g["qng"], g["kvng"]
        x_src = g["x_src"]
        HT, QA, KA, KPE, VA, QS, KS, VS, QM, KM, VM, MB = (g[k] for k in ("HT", "QA", "KA", "KPE", "VA", "QS", "KS", "VS", "QM", "KM", "VM", "MB"))
        FZ = list(self.fence)
        mark = A.mark()
        L = "L%d" % l

        Win = A.alloc([128, 8, NPROJ], BF16)
        Wuq = A.alloc([128, 6, 768], BF16)
        Wukv = A.alloc([128, 2, 1024], BF16)
        for k in range(8):
            B.dma("pool", Win[:, k, :], w_in[l, k * 128:(k + 1) * 128, 0:NPROJ], FZ, [("Win", k)])
        for k in range(6):
            src = w_uq[l, k * 128:(k + 1) * 128, :].rearrange("p (h r) -> p h r", h=8)
            B.dma("pool", Wuq[:, k, 0:512].rearrange("p (h r) -> p h r", h=8), src[:, :, 0:64], FZ, [("Wuq", k, 0)])
            B.dma("pool", Wuq[:, k, 512:768].rearrange("p (h r) -> p h r", h=8), src[:, :, 64:96], FZ, [("Wuq", k, 1)])
        for k in range(2):
            src = w_ukv[l, k * 128:(k + 1) * 128, :].rearrange("p (h r) -> p h r", h=8)
            B.dma("pool", Wukv[:, k, 0:512].rearrange("p (h r) -> p h r", h=8), src[:, :, 0:64], FZ, [("Wukv", k, 0)])
            B.dma("pool", Wukv[:, k, 512:1024].rearrange("p (h r) -> p h r", h=8), src[:, :, 64:128], FZ, [("Wukv", k, 1)])
        WinK = [("Win", k) for k in range(8)]
        WuqK = [("Wuq", k, i) for k in range(6) for i in range(2)]
        WukvK = [("Wukv", k, i) for k in range(2) for i in range(2)]

        xt = [A.alloc([128, D], F32) for _ in range(2)]
        junk = A.alloc([128, D], F32)
        xn = A.alloc([128, D], BF16)
        hT = [A.alloc([128, 8, 512], BF16) for _ in range(2)]
        st_ = A.alloc([128, 8], F32)
        latn = A.alloc([128, 1024], BF16)
        latT = A.alloc([128, 8, 128], BF16)
        qtok = A.alloc([128, 8, 96], BF16)
        rt = [A.alloc([128, 8, 16], F32) for _ in range(4)]
        kn = A.alloc([128, 512], BF16)
        vb = [A.alloc([128, 512], BF16) for _ in range(3)]
        kpe_t = A.alloc([128, 32], BF16)
        qs_t = A.alloc([128, 512], BF16); ks_t = A.alloc([128, 512], BF16)
        qm_f = A.alloc([128, 512], F32); km_f = A.alloc([128, 512], F32)
        km_b = A.alloc([128, 512], BF16)
        QmT = A.alloc([128, 4, 128], BF16)
        qm_b = A.alloc([128, 512], BF16)
        ones_b = A.alloc([128, 1], BF16)
        kmhi = A.alloc([128, 4, 32], BF16); kmlo = A.alloc([128, 4, 32], BF16); ktmp = A.alloc([128, 4, 1], F32)
        kmsum = A.alloc([128, 4, 16], F32)
        G = A.alloc([128, 8, 16], F32); m8 = A.alloc([128, 8, 8], F32); sel = A.alloc([128, 8, 16], F32); mk2 = A.alloc([128, 8, 16], F32)
        mbt = A.alloc([128, 8, 16], BF16)
        sQA = A.alloc([96, 8, 512], BF16); sKA = A.alloc([128, 4, 512], BF16); sKPE = A.alloc([32, 512], BF16)
        sQS = A.alloc([128, 4, 512], BF16); sKS = A.alloc([128, 4, 512], BF16)
        sQM = A.alloc([128, 4, 512], BF16); sKM = A.alloc([128, 4, 512], BF16); sMB = A.alloc([128, 512], BF16)

        B.memset("pool", kmsum, 0.0, [], ["kmsum"])
        B.memset("pool", ones_b, 1.0, [], ["ones_b"])
        B.memset("pool", kmhi, 0.0, [], ["kmhl"])
        B.memset("pool", kmlo, 0.0, [], ["kmhl"])
        acc_rr = [1, 2, 5]
        acc_i = [0]

        def next_acc():
            b = acc_rr[acc_i[0] % 3]
            acc_i[0] += 1
            return b

        ev_i = [0]

        def ev_eng():
            ev_i[0] += 1
            return "act" if ev_i[0] % 2 == 0 else "dve"

        def rope(x1, x2, o1, o2, cs, sn, n, rkeys, wkeys, h=8):
            if h > 1:
                c_b = cs.unsqueeze(1).to_broadcast([128, h, n]); s_b = sn.unsqueeze(1).to_broadcast([128, h, n])
                t = [r[:, 0:h, 0:n] for r in rt]
            else:
                c_b = cs; s_b = sn
                t = [r[:, 0, 0:n] for r in rt]
            B.tt("dve", t[0], x1, c_b, ALU.mult, rkeys, ["rt0"])
            B.tt("dve", t[1], x2, s_b, ALU.mult, rkeys, ["rt1"])
            B.tt("dve", t[2], x1, s_b, ALU.mult, rkeys, ["rt2"])
            B.tt("dve", t[3], x2, c_b, ALU.mult, rkeys, ["rt3"])
            B.tt("pool", o1, t[0], t[1], ALU.subtract, ["rt0", "rt1"], wkeys)
            B.tt("pool", o2, t[2], t[3], ALU.add, ["rt2", "rt3"], wkeys)

        for tt_ in self.debug.get('p1_tile_list', range(self.debug.get('p1_tiles', NT))):
            gi = tt_ // 4; i = tt_ % 4
            xb = xt[tt_ % 2]; xk = "xt%d" % (tt_ % 2)
            hb = hT[gi % 2]; hk = ("hT", gi % 2)
            tok = slice(i * 128, (i + 1) * 128)
            B.dma("sp", xb, x_src[tt_ * 128:(tt_ + 1) * 128, :], FZ + [("XL", tt_)], [xk])
            B.act(junk, xb, AF.Square, [xk], ["junk", "ss"], accum_out=st_[:, 0:1])
            B.act(st_[:, 1:2], st_[:, 0:1], AF.Sqrt, ["ss"], ["sd"], scale=1.0 / D, bias=EPS)
            B.recip(st_[:, 2:3], st_[:, 1:2], ["sd"], ["rstd"])
            B.ts("pool", xn, xb, st_[:, 2:3], None, ALU.mult, None, [xk, "rstd"], ["xn"])
            if self.debug.get('p1_cut', 99) <= 1:
                continue
            for j in range(8):
                B.tr(BKH[0][:, j * 128:(j + 1) * 128], xn[:, j * 128:(j + 1) * 128], ident_b, ["xn", "ident_b"], ["bk0"])
            for j in range(8):
                if j % 2 == 0:
                    B.act(hb[:, j, tok], BKH[0][:, j * 128:(j + 1) * 128], AF.Identity, ["bk0", "A1", ("modT", l)], [hk + (i,)],
                          scale=A1[:, l, j:j + 1], bias=modT[:, l, j:j + 1])
                else:
                    B.ts("dve", hb[:, j, tok], BKH[0][:, j * 128:(j + 1) * 128], A1[:, l, j:j + 1], modT[:, l, j:j + 1], ALU.mult, ALU.add,
                         ["bk0", "A1", ("modT", l)], [hk + (i,)])
            if self.debug.get('p1_cut', 99) <= 2:
                continue
            hki = [hk + (i,)]

            def proj(c0, n):
                b = next_acc()
                for k in range(8):
                    B.mm(BK[b][:, 0:n], hb[:, k, tok], Win[:, k, c0:c0 + n], k == 0, k == 7, hki + [("Win", k)], ["bk%d" % b])
                return b

            b0 = proj(0, 512)
            b1 = proj(512, 512)
            B.act(junk[:, 0:512], BK[b0], AF.Square, ["bk%d" % b0], ["junk", "ssq0"], accum_out=st_[:, 3:4])
            B.act(junk[:, 0:256], BK[b1][:, 0:256], AF.Square, ["bk%d" % b1], ["junk", "ssq1"], accum_out=st_[:, 4:5])
            B.act(junk[:, 256:512], BK[b1][:, 256:512], AF.Square, ["bk%d" % b1], ["junk", "sskv"], accum_out=st_[:, 5:6])
            B.tt("dve", st_[:, 3:4], st_[:, 3:4], st_[:, 4:5], ALU.add, ["ssq0", "ssq1"], ["ssq0"])
            B.act(st_[:, 6:7], st_[:, 3:4], AF.Sqrt, ["ssq0"], ["rq"], scale=1.0 / 768, bias=EPS)
            B.act(st_[:, 7:8], st_[:, 5:6], AF.Sqrt, ["sskv"], ["rkv"], scale=1.0 / 256, bias=EPS)
            B.recip(st_[:, 6:8], st_[:, 6:8], ["rq", "rkv"], ["rq", "rkv"])
            B.act(latn[:, 0:512], BK[b0], AF.Identity, ["bk%d" % b0, "rq"], [("latn", 0)], scale=st_[:, 6:7])
            B.ts("dve", latn[:, 512:768], BK[b1][:, 0:256], st_[:, 6:7], None, ALU.mult, None, ["bk%d" % b1, "rq"], [("latn", 1)])
            B.ts("dve", latn[:, 768:1024], BK[b1][:, 256:512], st_[:, 7:8], None, ALU.mult, None, ["bk%d" % b1, "rkv"], [("latn", 2)])
            for j in range(8):
                B.tr(BKH[4][:, j * 128:(j + 1) * 128], latn[:, j * 128:(j + 1) * 128], ident_b,
                     [("latn", 0), ("latn", 1), ("latn", 2), "ident_b"], ["bk4"])
            for j in range(8):
                gsc = qng[:, l, j:j + 1] if j < 6 else kvng[:, l, j - 6:j - 5]
                if j % 2 == 0:
                    B.act(latT[:, j, :], BKH[4][:, j * 128:(j + 1) * 128], AF.Identity, ["bk4", "qng", "kvng"], [("latT", j)], scale=gsc)
                else:
                    B.ts("dve", latT[:, j, :], BKH[4][:, j * 128:(j + 1) * 128], gsc, None, ALU.mult, None, ["bk4", "qng", "kvng"], [("latT", j)])
            if self.debug.get('p1_cut', 99) <= 3:
                continue
            latK = [("latT", j) for j in range(8)]
            bq0 = next_acc()
            for k in range(6):
                B.mm(BK[bq0], latT[:, k, :], Wuq[:, k, 0:512], k == 0, k == 5, latK + WuqK, ["bk%d" % bq0])
            B.copy("dve", qtok[:, :, 0:64], BK[bq0].rearrange("p (h r) -> p h r", h=8), ["bk%d" % bq0], [("qtok", 0)])
            bq1 = next_acc()
            for k in range(6):
                B.mm(BK[bq1][:, 0:256], latT[:, k, :], Wuq[:, k, 512:768], k == 0, k == 5, latK + WuqK, ["bk%d" % bq1])
            pe_v = BK[bq1][:, 0:256].rearrange("p (h r) -> p h r", h=8)
            rope(pe_v[:, :, 0:16], pe_v[:, :, 16:32], qtok[:, :, 64:80], qtok[:, :, 80:96], cosA[:, tt_, :], sinA[:, tt_, :], 16,
                 ["bk%d" % bq1, "cosA", "sinA"], [("qtok", 1)])
            bk0_ = next_acc()
            for k in range(2):
                B.mm(BK[bk0_], latT[:, 6 + k, :], Wukv[:, k, 0:512], k == 0, k == 1, latK + WukvK, ["bk%d" % bk0_])
            B.copy("act", kn, BK[bk0_], ["bk%d" % bk0_], ["kn"])
            bv_ = next_acc()
            for k in range(2):
                B.mm(BK[bv_], latT[:, 6 + k, :], Wukv[:, k, 512:1024], k == 0, k == 1, latK + WukvK, ["bk%d" % bv_])
            B.copy("act", vb[0], BK[bv_], ["bk%d" % bv_], ["vb0"])
            B.dma("sp", VA[tt_ * 128:(tt_ + 1) * 128, :], vb[0], ["vb0"], [("VA", tt_)])
            for h in range(8):
                B.tr(BKH[6][0:96, h * 128:(h + 1) * 128], qtok[:, h, :], ident_b, [("qtok", 0), ("qtok", 1), "ident_b"], ["bk6"])
            B.copy(ev_eng(), sQA[:, :, tok], BKH[6][0:96, :].rearrange("p (h t) -> p h t", h=8), ["bk6"], [("sQA", i)])
            for pr in range(4):
                B.tr(BKH[6][:, pr * 128:(pr + 1) * 128], kn[:, pr * 128:(pr + 1) * 128], ident_b, ["kn", "ident_b"], ["bk6"])
            B.copy(ev_eng(), sKA[:, :, tok], BKH[6][:, 0:512].rearrange("p (h t) -> p h t", h=4), ["bk6"], [("sKA", i)])
            bp = next_acc()
            for k in range(8):
                B.mm(BK[bp][:, 0:32], hb[:, k, tok], Win[:, k, OFF_KPE:OFF_KPE + 32], k == 0, k == 7, hki + [("Win", k)], ["bk%d" % bp])
            rope(BK[bp][:, 0:16], BK[bp][:, 16:32], kpe_t[:, 0:16], kpe_t[:, 16:32], cosA[:, tt_, :], sinA[:, tt_, :], 16,
                 ["bk%d" % bp, "cosA", "sinA"], ["kpe_t"], h=1)
            B.tr(BKH[6][0:32, 0:128], kpe_t, ident_b, ["kpe_t", "ident_b"], ["bk6"])
            B.copy(ev_eng(), sKPE[:, tok], BKH[6][0:32, 0:128], ["bk6"], [("sKPE", i)])

            if self.debug.get('p1_cut', 99) <= 4:
                continue
            b = proj(OFF_SBQ, 512)
            B.act(qs_t, BK[b], AF.Copy, ["bk%d" % b], ["qs_t"], scale=0.125)
            for pr in range(4):
                B.tr(BKH[6][:, pr * 128:(pr + 1) * 128], qs_t[:, pr * 128:(pr + 1) * 128], ident_b, ["qs_t", "ident_b"], ["bk6"])
            B.copy(ev_eng(), sQS[:, :, tok], BKH[6][:, 0:512].rearrange("p (h t) -> p h t", h=4), ["bk6"], [("sQS", i)])
            b = proj(OFF_SBK, 512)
            B.copy("act", ks_t, BK[b], ["bk%d" % b], ["ks_t"])
            for pr in range(4):
                B.tr(BKH[6][:, pr * 128:(pr + 1) * 128], ks_t[:, pr * 128:(pr + 1) * 128], ident_b, ["ks_t", "ident_b"], ["bk6"])
            B.copy(ev_eng(), sKS[:, :, tok], BKH[6][:, 0:512].rearrange("p (h t) -> p h t", h=4), ["bk6"], [("sKS", i)])
            b = proj(OFF_SBV, 512)
            B.copy("act", vb[1], BK[b], ["bk%d" % b], ["vb1"])
            B.dma("sp", VS[tt_ * 128:(tt_ + 1) * 128, :], vb[1], ["vb1"], [("VS", tt_)])

            if self.debug.get('p1_cut', 99) <= 5:
                continue
            b = proj(OFF_MBK, 512)
            B.copy("act", km_f, BK[b], ["bk%d" % b], ["km_f"])
            kv3 = km_f.rearrange("p (h r) -> p h r", h=8)
            rope(kv3[:, :, 0:8], kv3[:, :, 8:16], kv3[:, :, 0:8], kv3[:, :, 8:16], cosB[:, tt_, :], sinB[:, tt_, :], 8,
                 ["km_f", "cosB", "sinB"], ["km_f"])
            B.copy("pool", km_b, km_f, ["km_f"], ["km_b"])
            for pr in range(4):
                B.tr(BKH[6][:, pr * 128:(pr + 1) * 128], km_b[:, pr * 128:(pr + 1) * 128], ident_b, ["km_b", "ident_b"], ["bk6"])
            B.copy(ev_eng(), sKM[:, :, tok], BKH[6][:, 0:512].rearrange("p (h t) -> p h t", h=4), ["bk6"], [("sKM", i)])
            nblk = tt_ // 2
            bcs = next_acc()
            for pr in range(4):
                B.mm(BK[bcs][:, pr:pr + 1], km_b[:, pr * 128:(pr + 1) * 128], ones_b, True, True, ["km_b", "ones_b"], ["bk%d" % bcs])
            if tt_ % 2 == 0:
                B.copy("dve", kmsum[:, :, nblk], BK[bcs][:, 0:4], ["bk%d" % bcs], ["kmsum"])
            else:
                B.tt("dve", kmsum[:, :, nblk], kmsum[:, :, nblk], BK[bcs][:, 0:4], ALU.add, ["bk%d" % bcs, "kmsum"], ["kmsum"])
                for (r0_, coff) in ((0, 0), (64, 16)):
                    rs_ = slice(r0_, r0_ + 64)
                    B.copy("dve", kmhi[rs_, :, coff + nblk], kmsum[rs_, :, nblk], ["kmsum"], ["kmhl"])
                    B.tt("dve", ktmp[rs_, :, 0], kmsum[rs_, :, nblk], kmhi[rs_, :, coff + nblk], ALU.subtract, ["kmsum", "kmhl"], ["ktmp"])
                    B.copy("dve", kmlo[rs_, :, coff + nblk], ktmp[rs_, :, 0], ["ktmp"], ["kmhl"])

            if self.debug.get('p1_cut', 99) <= 6:
                continue
            b = proj(OFF_MBQ, 512)
            B.copy("act", qm_f, BK[b], ["bk%d" % b], ["qm_f"])
            qv3 = qm_f.rearrange("p (h r) -> p h r", h=8)
            rope(qv3[:, :, 0:8], qv3[:, :, 8:16], qv3[:, :, 0:8], qv3[:, :, 8:16], cosB[:, tt_, :], sinB[:, tt_, :], 8,
                 ["qm_f", "cosB", "sinB"], ["qm_f"])
            B.copy("pool", qm_b, qm_f, ["qm_f"], ["qm_b"])
            for pr in range(4):
                B.tr(BKH[6][:, pr * 128:(pr + 1) * 128], qm_b[:, pr * 128:(pr + 1) * 128], ident_b, ["qm_b", "ident_b"], ["bk6"])
            B.copy("dve", QmT, BKH[6][:, 0:512].rearrange("p (h t) -> p h t", h=4), ["bk6"], ["QmT"])
            B.copy("dve", sQM[:, :, tok], BKH[6][:, 0:512].rearrange("p (h t) -> p h t", h=4), ["bk6"], [("sQM", i)])
            b = proj(OFF_MBV, 512)
            B.copy("act", vb[2], BK[b], ["bk%d" % b], ["vb2"])
            B.dma("sp", VM[tt_ * 128:(tt_ + 1) * 128, :], vb[2], ["vb2"], [("VM", tt_)])

            if self.debug.get('p1_cut', 99) <= 7:
                continue
            cur = tt_ // 2
            gcut = self.debug.get("gate_cut", 99)
            if cur >= 4:
                bg = next_acc()
                for pr in range(4):
                    B.mm(BK[bg][:, pr * 32:(pr + 1) * 32], QmT[:, pr, :], kmhi[:, pr, :], True, False, ["QmT", "kmhl"], ["bk%d" % bg])
                    B.mm(BK[bg][:, pr * 32:(pr + 1) * 32], QmT[:, pr, :], kmlo[:, pr, :], False, True, ["QmT", "kmhl"], ["bk%d" % bg])
                if gcut >= 2:
                    B.memset("pool", G, -1e30, [], ["G"])
                    B.copy("dve", G[:, :, 0:cur], BK[bg][:, 0:128].rearrange("p (h n) -> p h n", h=8)[:, :, 0:cur], ["bk%d" % bg], ["G"])
                if self.debug.get("dump_G") == tt_:
                    if "GD" not in self.__dict__:
                        self.GD = self.nc.dram_tensor("GDBG", [128, 128], F32, kind="ExternalOutput").ap()
                        self.GD2 = self.nc.dram_tensor("GDBG2", [128, 64], F32, kind="ExternalOutput").ap()
                        self.GD3 = self.nc.dram_tensor("GDBG3", [128, 512], F32, kind="ExternalOutput").ap()
                    B.dma("sp", self.GD, G.rearrange("p h n -> p (h n)"), ["G"], ["GD"])
                    B.dma("sp", self.GD2, kmsum.rearrange("p h n -> p (h n)"), ["kmsum"], ["GD2"])
                    B.dma("sp", self.GD3, QmT.rearrange("p h n -> p (h n)"), ["QmT"], ["GD3"])
                if gcut >= 3:
                    G2 = sel
                    B.P.add("dve", lambda e: e.reduce_max(out=m8[:, :, 0], in_=G, axis=AX.X), ["G"], ["m8"])
                    B.tt("dve", G2, G, m8[:, :, 0:1].to_broadcast([128, 8, 16]), ALU.is_ge, ["G", "m8"], ["sel"])
                    B.P.add("dve", lambda e: e.scalar_tensor_tensor(out=G2, in0=G2, scalar=-3e30, in1=G, op0=ALU.mult, op1=ALU.add), ["sel", "G"], ["sel"])
                    B.P.add("dve", lambda e: e.reduce_max(out=m8[:, :, 1], in_=G2, axis=AX.X), ["sel"], ["m8"])
                    B.tt("dve", mk2, G2, m8[:, :, 1:2].to_broadcast([128, 8, 16]), ALU.is_ge, ["sel", "m8"], ["mk2"])
                    B.P.add("dve", lambda e: e.scalar_tensor_tensor(out=G2, in0=mk2, scalar=-3e30, in1=G2, op0=ALU.mult, op1=ALU.add), ["mk2", "sel"], ["sel"])
                    B.P.add("dve", lambda e: e.reduce_max(out=m8[:, :, 2], in_=G2, axis=AX.X), ["sel"], ["m8"])
                if gcut >= 4:
                    B.tt("dve", sel, G, m8[:, :, 2:3].to_broadcast([128, 8, 16]), ALU.is_ge, ["G", "m8"], ["sel"])
                if gcut >= 5:
                    B.ts("dve", mbt, sel, -MASKNEG, MASKNEG, ALU.mult, ALU.add, ["sel"], ["mbt"])
                B.memset("pool", mbt[:, :, cur:cur + 1], 0.0, [], ["mbt"])
            else:
                B.memset("pool", mbt, MASKNEG, [], ["mbt"])
                B.memset("pool", mbt[:, :, 0:cur + 1], 0.0, [], ["mbt"])
            B.tr(BKH[6][:, 0:128], mbt.rearrange("p h n -> p (h n)"), ident_b, ["mbt", "ident_b"], ["bk6"])
            B.copy(ev_eng(), sMB[:, tok], BKH[6][:, 0:128], ["bk6"], [("sMB", i)])

            if i == 3:
                cs = slice(gi * 512, (gi + 1) * 512)
                B.dma("sp", HT[:, cs].rearrange("(j p) t -> p j t", p=128), hb, [hk + (ii,) for ii in range(4)], [("HT", gi)])
                B.dma("sp", QA[:, cs].rearrange("(h r) t -> r h t", r=96), sQA, [("sQA", ii) for ii in range(4)], [("QA", gi)])
                B.dma("sp", KA[:, cs].rearrange("(c p) t -> p c t", p=128), sKA, [("sKA", ii) for ii in range(4)], [("KA", gi)])
                B.dma("sp", KPE[:, cs], sKPE, [("sKPE", ii) for ii in range(4)], [("KPE", gi)])
                B.dma("sp", QS[:, cs].rearrange("(c p) t -> p c t", p=128), sQS, [("sQS", ii) for ii in range(4)], [("QS", gi)])
                B.dma("sp", KS[:, cs].rearrange("(c p) t -> p c t", p=128), sKS, [("sKS", ii) for ii in range(4)], [("KS", gi)])
                B.dma("sp", QM[:, cs].rearrange("(c p) t -> p c t", p=128), sQM, [("sQM", ii) for ii in range(4)], [("QM", gi)])
                B.dma("sp", KM[:, cs].rearrange("(c p) t -> p c t", p=128), sKM, [("sKM", ii) for ii in range(4)], [("KM", gi)])
                B.dma("sp", MB[:, cs], sMB, [("sMB", ii) for ii in range(4)], [("MB", gi)])

        keys = [("hT", 0, ii) for ii in range(4)] + [("hT", 1, ii) for ii in range(4)]
        keys += ["xt0", "xt1", "junk", "xn", "kn", "vb0", "vb1", "vb2", "kpe_t", "qs_t", "ks_t", "qm_f", "km_f", "km_b", "QmT", "kmsum", "G", "m8",
                 "sel", "mbt", "rt0", "rt1", "rt2", "rt3", ("qtok", 0), ("qtok", 1), ("latn", 0), ("latn", 1), ("latn", 2)]
        keys += [("latT", j) for j in range(8)] + WinK + WuqK + WukvK
        for nm in ("sQA", "sKA", "sKPE", "sQS", "sKS", "sQM", "sKM", "sMB"):
            keys += [(nm, ii) for ii in range(4)]
        keys += ["bk%d" % b for b in range(8)]
        self.barrier(keys, "p1_%d" % l)
        A.release(mark)

    def phase2(self, l, A, BK, BKH, env):
        B = self; P = self.P
        g = env
        tri_neg, neg_ones, zeros_b = g["tri_neg"], g["neg_ones"], g["zeros_b"]
        QA, KA, KPE, VA, QS, KS, VS, QM, KM, VM, MB, OH, OT = (g[k] for k in ("QA", "KA", "KPE", "VA", "QS", "KS", "VS", "QM", "KM", "VM", "MB", "OH", "OT"))
        mark = A.mark()
        QT = [A.alloc([96, S], BF16) for _ in range(2)]
        KT = [A.alloc([96, S], BF16) for _ in range(2)]
        V = [A.alloc([128, NT, 128], BF16) for _ in range(2)]
        Pt = [A.alloc([128, 512], BF16) for _ in range(4)]
        U = [A.alloc([128, 512], F32) for _ in range(3)]
        Lb = [A.alloc([128, 512], BF16) for _ in range(3)]
        R = A.alloc([128, 512], BF16)
        rd = A.alloc([64, 512], F32)
        ost = [A.alloc([64, S], BF16) for _ in range(2)]
        for p in range(2):
            B.memset("pool", V[p][:, :, 64:128], 1.0, [], [("Vones", p)])
        allg = list(range(NG)); allt = list(range(NT))
        units = [(br, h) for br in self.debug.get("p2_branches", (0, 1, 2)) for h in range(self.debug.get("p2_heads", 8))]
        cnt = {"s": 0, "p": 0, "e": 0, "u": 0}
        for u, (br, h) in enumerate(units):
            par = u % 2
            qt, kt_, vv, os_ = QT[par], KT[par], V[par], ost[par]
            qk = [("QT", par, 0), ("QT", par, 1)]; kk = [("KT", par, 0), ("KT", par, 1)]; vk = [("V", par), ("Vones", par)]
            if br == 0:
                dk = 96; scale = 1.0 / math.sqrt(96.0)
                B.dma("pool", qt[0:96, :], QA[h * 96:(h + 1) * 96, :], [("QA", gg) for gg in allg], qk)
                B.dma("pool", kt_[0:64, :], KA[h * 64:(h + 1) * 64, :], [("KA", gg) for gg in allg], [kk[0]])
                B.dma("pool", kt_[64:96, :], KPE[:, :], [("KPE", gg) for gg in allg], [kk[1]])
                B.dma("pool", vv[:, :, 0:64], VA[:, h * 64:(h + 1) * 64].rearrange("(j p) d -> p j d", p=128), [("VA", t) for t in allt], [vk[0]])
            elif br == 1:
                dk = 64; scale = 1.0
                B.dma("pool", qt[0:64, :], QS[h * 64:(h + 1) * 64, :], [("QS", gg) for gg in allg], qk)
                B.dma("pool", kt_[0:64, :], KS[h * 64:(h + 1) * 64, :], [("KS", gg) for gg in allg], kk)
                B.dma("pool", vv[:, :, 0:64], VS[:, h * 64:(h + 1) * 64].rearrange("(j p) d -> p j d", p=128), [("VS", t) for t in allt], [vk[0]])
            else:
                dk = 80; scale = 0.125
                B.dma("pool", qt[0:64, :], QM[h * 64:(h + 1) * 64, :], [("QM", gg) for gg in allg], [qk[0]])
                B.dma("pool", qt[64:80, :], MB[h * 16:(h + 1) * 16, :], [("MB", gg) for gg in allg], [qk[1]])
                B.dma("pool", kt_[0:64, :], KM[h * 64:(h + 1) * 64, :], [("KM", gg) for gg in allg], [kk[0]])
                B.dma("pool", kt_[64:80, :], OH[:, :], ["OH"], [kk[1]])
                B.dma("pool", vv[:, :, 0:64], VM[:, h * 64:(h + 1) * 64].rearrange("(j p) d -> p j d", p=128), [("VM", t) for t in allt], [vk[0]])
            for gq in range(self.debug.get("p2_groups", NG)):
                ob = 5 + (gq % 2); obk = "bk%d" % ob
                qs0 = gq * 512
                if br != 1:
                    nkt = 4 * gq + 4
                    tiles = []
                    for kt in range(nkt):
                        i = kt - 4 * gq
                        c0 = 128 * i if i > 0 else 0
                        tiles.append((kt, i, c0))

                    def stA(t):
                        kt, i, c0 = t
                        sbk = cnt["s"] % 5; cnt["s"] += 1
                        pi = cnt["p"] % 4; cnt["p"] += 1
                        pb = Pt[pi]; pk = ("Pt", pi)
                        B.mm(BK[sbk][:, c0:512], kt_[0:dk, kt * 128:(kt + 1) * 128], qt[0:dk, qs0 + c0:qs0 + 512], True, True, qk + kk, ["bk%d" % sbk])
                        B.act(pb[:, c0:512], BK[sbk][:, c0:512], AF.Exp, ["bk%d" % sbk], [pk], scale=scale)
                        if i >= 0:
                            B.asel(pb[:, c0:c0 + 128], pb[:, c0:c0 + 128], [[1, 128]], ALU.is_ge, 0.0, 0, -1, [pk], [pk])
                        return pb, pk

                    def stE(t, pbk):
                        kt, i, c0 = t
                        pb, pk = pbk
                        B.mm(BK[ob][:, c0:512], vv[:, kt, :], pb[:, c0:512], kt == 0, kt == nkt - 1, vk + [pk], [obk])

                    prev = stA(tiles[0])
                    for si in range(nkt):
                        nxt = stA(tiles[si + 1]) if si + 1 < nkt else None
                        stE(tiles[si], prev)
                        prev = nxt
                    B.recip(rd[0:64, :], BK[ob][64:128, :], [obk], ["rd"])
                    B.tt("dve", os_[0:64, qs0:qs0 + 512], BK[ob][0:64, :], rd[0:64, :], ALU.mult, [obk, "rd"], [("ost", par)])
                else:
                    B.memset("pool", R, 0.0, [], ["R"])
                    B.mm(BK[ob][0:64, :], zeros_b[:, 0:64], R, True, False, ["zeros_b", "R"], [obk])
                    tiles = []
                    for kt in range(4 * gq + 3, -1, -1):
                        i = kt - 4 * gq
                        c0 = 128 * i if i > 0 else 0
                        tiles.append((kt, i, c0))
                    n_t = len(tiles)

                    def sbA(t):
                        kt, i, c0 = t
                        zbk = cnt["s"] % 3; cnt["s"] += 1
                        ui = cnt["u"] % 3; cnt["u"] += 1
                        ksl = kt_[0:64, kt * 128:(kt + 1) * 128]; qsl = qt[0:64, qs0 + c0:qs0 + 512]
                        B.mm(BK[zbk][:, c0:512], ksl, qsl, True, True, qk + kk, ["bk%d" % zbk])
                        B.act(U[ui][:, c0:512], BK[zbk][:, c0:512], AF.Exp, ["bk%d" % zbk], [("U", ui)])
                        B.act(Lb[ui][:, c0:512], U[ui][:, c0:512], AF.Ln, [("U", ui)], [("Lb", ui)], bias=1.0)
                        if i >= 0:
                            B.asel(Lb[ui][:, c0:c0 + 128], Lb[ui][:, c0:c0 + 128], [[1, 128]], ALU.is_gt, 0.0, 0, -1, [("Lb", ui)], [("Lb", ui)])
                        return ui

                    def sbC(t, ui, first):
                        kt, i, c0 = t
                        ebk = 3 + cnt["e"] % 2; cnt["e"] += 1
                        pi = cnt["p"] % 4; cnt["p"] += 1
                        pb = Pt[pi]; pk = ("Pt", pi)
                        ksl = kt_[0:64, kt * 128:(kt + 1) * 128]; qsl = qt[0:64, qs0 + c0:qs0 + 512]
                        B.mm(BK[ebk][:, c0:512], ksl, qsl, True, False, qk + kk, ["bk%d" % ebk])
                        B.mm(BK[ebk][:, c0:512], tri_neg, Lb[ui][:, c0:512], False, first, ["tri_neg", ("Lb", ui)], ["bk%d" % ebk])
                        if not first:
                            B.mm(BK[ebk][:, c0:512], neg_ones, R[:, c0:512], False, True, ["neg_ones", "R"], ["bk%d" % ebk])
                        B.act(pb[:, c0:512], BK[ebk][:, c0:512], AF.Exp, ["bk%d" % ebk], [pk])
                        if i >= 0:
                            B.asel(pb[:, c0:c0 + 128], pb[:, c0:c0 + 128], [[1, 128]], ALU.is_gt, 0.0, 0, -1, [pk], [pk])
                        if kt > 0:
                            B.tt("dve", R[:, c0:512], R[:, c0:512], Lb[ui][:, c0:512], ALU.add, ["R", ("Lb", ui)], ["R"])
                        return pb, pk

                    def sbE(t, pbk):
                        kt, i, c0 = t
                        pb, pk = pbk
                        B.mm(BK[ob][0:64, c0:512], vv[:, kt, 0:64], pb[:, c0:512], False, kt == 0, vk + [pk], [obk])

                    uis = [None] * n_t
                    pbs = [None] * n_t
                    uis[0] = sbA(tiles[0])
                    for si in range(n_t + 1):
                        if si + 1 < n_t:
                            uis[si + 1] = sbA(tiles[si + 1])
                        if si < n_t:
                            pbs[si] = sbC(tiles[si], uis[si], si == 0)
                        if si >= 1:
                            sbE(tiles[si - 1], pbs[si - 1])
                    B.copy("dve", os_[0:64, qs0:qs0 + 512], BK[ob][0:64, :], [obk], [("ost", par)])
            B.dma("sp", OT[br, h * 64:(h + 1) * 64, :], os_[0:64, :], [("ost", par)], [("OT", br, h)])
        self.barrier()
        A.release(mark)

    def phase3(self, l, A, BK, BKH, env):
        B = self; P = self.P
        g = env
        IN = self.IN
        w_in = IN["w_in"]; w_out = IN["w_out"]
        w_o = [IN["w_o_mla"], IN["w_o_sb"], IN["w_o_moba"]]
        HT, OT, XM, modd, gbc = g["HT"], g["OT"], g["XM"], g["modd"], g["gbc"]
        x_src = g["x_src"]
        mark = A.mark()
        Wg = A.alloc([128, 8, 3072], BF16)
        Wo = [A.alloc([128, 4, D], BF16) for _ in range(3)]
        Wout = A.alloc([128, 8, D], BF16)
        for k in range(8):
            B.dma("pool", Wg[:, k, :], w_in[l, k * 128:(k + 1) * 128, OFF_G:DIN], [], [("Wg", k)])
            B.dma("pool", Wout[:, k, :], w_out[l, k * 128:(k + 1) * 128, :], [], [("Wout", k)])
        for br in range(3):
            for k in range(4):
                B.dma("pool", Wo[br][:, k, :], w_o[br][l, k * 128:(k + 1) * 128, :], [], [("Wo", br, k)])
        B.dma("sp", gbc, modd[l, 16 * 128:24 * 128].partition_broadcast(128), [("modd", l)], ["gbc"])
        WgK = [("Wg", k) for k in range(8)]; WoutK = [("Wout", k) for k in range(8)]
        hT = [A.alloc([128, 8, 512], BF16) for _ in range(2)]
        oT = [[A.alloc([128, 4, 512], BF16) for _ in range(3)] for _ in range(2)]
        sig = [A.alloc([128, 512], F32) for _ in range(2)]
        prod = [A.alloc([128, 512], F32) for _ in range(2)]
        macc = A.alloc([128, 512], F32)
        mT = A.alloc([128, 8, 512], BF16)
        xt = [A.alloc([128, D], F32) for _ in range(2)]
        xo = [A.alloc([128, D], F32) for _ in range(2)]
        tmp = [A.alloc([128, 512], F32) for _ in range(2)]
        c = {"g": 0, "b": 0, "s": 0, "o": 0, "t": 0}
        for gi in range(NG):
            par = gi % 2
            cs = slice(gi * 512, (gi + 1) * 512)
            B.dma("sp", hT[par], HT[:, cs].rearrange("(j p) t -> p j t", p=128), [("HT", gi)], [("hT3", par)])
            for br in range(3):
                B.dma("sp", oT[par][br], OT[br][:, cs].rearrange("(k p) t -> p k t", p=128), [("OT", br, h) for h in range(8)], [("oT3", par, br)])
            for j in range(8):
                for br in range(3):
                    gb = c["g"] % 3; c["g"] += 1
                    bb = 3 + c["b"] % 3; c["b"] += 1
                    si = c["s"] % 2; c["s"] += 1
                    for k in range(8):
                        B.mm(BK[gb], Wg[:, k, br * D + j * 128:br * D + (j + 1) * 128], hT[par][:, k, :], k == 0, k == 7,
                             [("Wg", k), ("hT3", par)], ["bk%d" % gb])
                    for k in range(4):
                        B.mm(BK[bb], Wo[br][:, k, j * 128:(j + 1) * 128], oT[par][br][:, k, :], k == 0, k == 3,
                             [("Wo", br, k), ("oT3", par, br)], ["bk%d" % bb])
                    B.act(sig[si], BK[gb], AF.Sigmoid, ["bk%d" % gb], [("sig", si)])
                    if br == 0:
                        B.tt("dve", macc, BK[bb], sig[si], ALU.mult, ["bk%d" % bb, ("sig", si)], ["macc"])
                    else:
                        B.tt("dve", prod[si], BK[bb], sig[si], ALU.mult, ["bk%d" % bb, ("sig", si)], [("prod", si)])
                        if br == 1:
                            B.tt("pool", macc, macc, prod[si], ALU.add, ["macc", ("prod", si)], ["macc"])
                        else:
                            B.tt("pool", mT[:, j, :], macc, prod[si], ALU.add, ["macc", ("prod", si)], [("mT", j)])
            for i in range(4):
                tt_ = gi * 4 + i
                xb = xt[tt_ % 2]; xk = ("xt3", tt_ % 2)
                yb = xo[tt_ % 2]; yk = ("xo3", tt_ % 2)
                B.dma("sp", xb, x_src[tt_ * 128:(tt_ + 1) * 128, :], [("XL", tt_)], [xk])
                for half in range(2):
                    ob = 6 + c["o"] % 2; c["o"] += 1
                    ti = c["t"] % 2; c["t"] += 1
                    hs = slice(half * 512, (half + 1) * 512)
                    for k in range(8):
                        B.mm(BK[ob], mT[:, k, i * 128:(i + 1) * 128], Wout[:, k, hs], k == 0, k == 7, [("mT", k), ("Wout", k)], ["bk%d" % ob])
                    B.tt("dve", tmp[ti], BK[ob], gbc[:, hs], ALU.mult, ["bk%d" % ob, "gbc"], [("tmp3", ti)])
                    B.tt("pool", yb[:, hs], xb[:, hs], tmp[ti], ALU.add, [xk, ("tmp3", ti)], [yk + (half,)])
                B.dma("sp", XM[tt_ * 128:(tt_ + 1) * 128, :], yb, [yk + (0,), yk + (1,)], [("XM", tt_)])
        self.barrier()
        A.release(mark)

    def phase4(self, l, A, BK, BKH, env, last):
        B = self; P = self.P
        g = env
        IN = self.IN
        w_ff1 = IN["w_ff1"]; w_ff2 = IN["w_ff2"]
        XM, XL, modd, gbc, fing_bc, y_out = g["XM"], g["XL"], g["modd"], g["gbc"], g["fing_bc"], g["y_out"]
        ident_b, modT, A2 = g["ident_b"], g["modT"], g["A2"]
        mark = A.mark()
        Wf1 = A.alloc([128, 8, DFF], BF16)
        Wf2 = A.alloc([128, 32, D], BF16)
        for k in range(8):
            B.dma("pool", Wf1[:, k, :], w_ff1[l, k * 128:(k + 1) * 128, :], [], [("Wf1", k)])
        for k in range(32):
            B.dma("pool", Wf2[:, k, :], w_ff2[l, k * 128:(k + 1) * 128, :], [], [("Wf2", k)])
        B.dma("sp", gbc, modd[l, 40 * 128:48 * 128].partition_broadcast(128), [("modd", l)], ["gbc"])
        GT = 256
        xt = [A.alloc([128, D], F32) for _ in range(2)]
        xn = A.alloc([128, D], BF16)
        st_ = A.alloc([128, 8], F32)
        h2T = A.alloc([128, 8, GT], BF16)
        hff = A.alloc([128, 32, GT], BF16)
        rl = [A.alloc([128, GT], F32) for _ in range(2)]
        xo = [A.alloc([128, D], F32) for _ in range(2)]
        tmp = [A.alloc([128, 512], F32) for _ in range(2)]
        c = {"a": 0, "r": 0, "o": 0, "t": 0}
        for gi in range(S // GT):
            for i in range(GT // 128):
                tt_ = gi * (GT // 128) + i
                xb = xt[i]; xk = ("xt4", i)
                tok = slice(i * 128, (i + 1) * 128)
                B.dma("sp", xb, XM[tt_ * 128:(tt_ + 1) * 128, :], [("XM", tt_)], [xk])
                B.act(xo[i], xb, AF.Square, [xk], [("xo4", i, 0), ("xo4", i, 1), "ss4"], accum_out=st_[:, 0:1])
                B.act(st_[:, 1:2], st_[:, 0:1], AF.Sqrt, ["ss4"], ["sd4"], scale=1.0 / D, bias=EPS)
                B.recip(st_[:, 2:3], st_[:, 1:2], ["sd4"], ["rstd4"])
                B.ts("pool", xn, xb, st_[:, 2:3], None, ALU.mult, None, [xk, "rstd4"], ["xn4"])
                for j in range(8):
                    B.tr(BKH[0][:, j * 128:(j + 1) * 128], xn[:, j * 128:(j + 1) * 128], ident_b, ["xn4", "ident_b"], ["bk0"])
                for j in range(8):
                    if j % 2 == 0:
                        B.act(h2T[:, j, tok], BKH[0][:, j * 128:(j + 1) * 128], AF.Identity, ["bk0", "A2", ("modT", l)], [("h2T", i)],
                              scale=A2[:, l, j:j + 1], bias=modT[:, l, 24 + j:25 + j])
                    else:
                        B.ts("dve", h2T[:, j, tok], BKH[0][:, j * 128:(j + 1) * 128], A2[:, l, j:j + 1], modT[:, l, 24 + j:25 + j], ALU.mult, ALU.add,
                             ["bk0", "A2", ("modT", l)], [("h2T", i)])
            hk = [("h2T", i) for i in range(GT // 128)]
            for cc in range(32):
                ab = 1 + c["a"] % 4; c["a"] += 1
                ri = c["r"] % 2; c["r"] += 1
                for k in range(8):
                    B.mm(BK[ab][:, 0:GT], Wf1[:, k, cc * 128:(cc + 1) * 128], h2T[:, k, :], k == 0, k == 7, [("Wf1", k)] + hk, ["bk%d" % ab])
                B.act(rl[ri], BK[ab][:, 0:GT], AF.Relu, ["bk%d" % ab], [("rl", ri)])
                B.tt("pool" if cc % 2 == 0 else "dve", hff[:, cc, :], rl[ri], rl[ri], ALU.mult, [("rl", ri)], [("hff", cc)])
            hfk = [("hff", cc) for cc in range(32)]
            for i in range(GT // 128):
                tt_ = gi * (GT // 128) + i
                xb = xt[i]; xk = ("xt4", i)
                yb = xo[i]
                for half in range(2):
                    ob = 5 + c["o"] % 3; c["o"] += 1
                    ti = c["t"] % 2; c["t"] += 1
                    hs = slice(half * 512, (half + 1) * 512)
                    for k in range(32):
                        B.mm(BK[ob], hff[:, k, i * 128:(i + 1) * 128], Wf2[:, k, hs], k == 0, k == 31, hfk + [("Wf2", k)], ["bk%d" % ob])
                    B.tt("dve", tmp[ti], BK[ob], gbc[:, hs], ALU.mult, ["bk%d" % ob, "gbc"], [("tmp4", ti)])
                    B.tt("pool", yb[:, hs], xb[:, hs], tmp[ti], ALU.add, [xk, ("tmp4", ti)], [("xo4", i, half)])
                yk = [("xo4", i, 0), ("xo4", i, 1)]
                if not last:
                    B.dma("sp", XL[tt_ * 128:(tt_ + 1) * 128, :], yb, yk, [("XL", tt_)])
                else:
                    B.act(xn.bitcast(F32)[:, 0:512] if False else tmp[0], yb[:, 0:512], AF.Square, yk, [("tmp4", 0), "fs0"], accum_out=st_[:, 3:4])
                    B.act(tmp[1], yb[:, 512:1024], AF.Square, yk, [("tmp4", 1), "fs1"], accum_out=st_[:, 4:5])
                    B.tt("dve", st_[:, 3:4], st_[:, 3:4], st_[:, 4:5], ALU.add, ["fs0", "fs1"], ["fs0"])
                    B.act(st_[:, 5:6], st_[:, 3:4], AF.Sqrt, ["fs0"], ["fsd"], scale=1.0 / D, bias=EPS)
                    B.recip(st_[:, 6:7], st_[:, 5:6], ["fsd"], ["frs"])
                    B.ts("pool", yb, yb, st_[:, 6:7], None, ALU.mult, None, yk + ["frs"], yk)
                    B.tt("pool", yb, yb, fing_bc, ALU.mult, yk + ["fing"], yk)
                    B.dma("sp", y_out[tt_ * 128:(tt_ + 1) * 128, :], yb, yk, [("Y", tt_)])
        self.barrier()
        A.release(mark)


_W_NAMES = ("w_ada", "b_ada", "norm1_g", "norm2_g", "w_in", "q_norm_g", "w_uq", "kv_norm_g", "w_ukv",
            "w_o_mla", "w_o_sb", "w_o_moba", "w_out", "w_ff1", "w_ff2", "final_norm_g")


def make_in_maps(inputs, declared=None):
    names = list(declared) if declared is not None else ["x", "c", "pos"] + list(_W_NAMES)
    src = {"x": ("x", np.float32), "c": ("c", np.float32), "pos": ("positions", np.int32)}
    shared = {}
    maps = [dict() for _ in range(8)]
    for n in names:
        if n in src:
            key, dt = src[n]
            arr = np.ascontiguousarray(np.asarray(inputs[key], dtype=dt))
            for b in range(8):
                maps[b][n] = arr[b]
        else:
            arr = np.ascontiguousarray(np.asarray(inputs[n], dtype=np.float32))
            for b in range(8):
                maps[b][n] = arr
    return maps


def kernel(**inputs):
    bld = Builder()
    nc = bld.build()
    res = run_bass_kernel_spmd(nc, make_in_maps(inputs, bld.declared), core_ids=list(range(8)))
    return np.stack([np.asarray(r["y"], dtype=np.float32) for r in res.results], axis=0)
```

```python
import math
from contextlib import ExitStack

import numpy as np
import concourse.bass as bass
import concourse.mybir as mybir
from concourse.bass_utils import run_bass_kernel_spmd

F32 = mybir.dt.float32
BF16 = mybir.dt.bfloat16
I32 = mybir.dt.int32
AF = mybir.ActivationFunctionType
ALU = mybir.AluOpType
AX = mybir.AxisListType

D = 1024
S = 4096
NT = S // 128
NG = S // 512
DEPTH = 2
DIN = 7200
DFF = 4096
EPS = 1e-6
THETA = 500000.0
OFF_QLAT, OFF_CKV, OFF_KPE = 0, 768, 1024
OFF_SBQ, OFF_SBK, OFF_SBV = 1056, 1568, 2080
OFF_MBQ, OFF_MBK, OFF_MBV = 2592, 3104, 3616
OFF_G = 4128
NPROJ = 4128
MASKNEG = -30000.0

N_DMA_SEMS = 24
SAME_ENGINE_SYNC = True


class Op:
    __slots__ = ("idx", "eng", "fn", "dma", "deps", "sem", "val", "inc", "waits", "known", "raw")

    def __init__(self, idx, eng, fn, dma):
        self.idx = idx; self.eng = eng; self.fn = fn; self.dma = dma
        self.deps = (); self.sem = None; self.val = 0; self.inc = False
        self.waits = (); self.known = None; self.raw = ()


class Prog:
    ENGS = ("sp", "act", "dve", "pool", "pe")

    def __init__(self, nc):
        self.nc = nc
        self.ops = []
        self.last_w = {}
        self.readers = {}
        self.dma_rr = {e: 0 for e in self.ENGS}
        self.dma_last = {}
        self.last_eng = {}

    def barrier(self):
        deps = list(self.last_eng.values()) + list(self.dma_last.values())
        for e in self.ENGS:
            op = self.add(e, None)
            dd = {d.idx: d for d in op.deps}
            for d in deps:
                dd[d.idx] = d
            op.deps = list(dd.values())

    def add(self, eng, fn, reads=(), writes=(), dma=False):
        op = Op(len(self.ops), eng, fn, dma)
        excl = [k for k in reads if isinstance(k, str) and k.startswith("bk")]
        if excl:
            writes = list(writes) + [k for k in excl if k not in writes]
            reads = [k for k in reads if k not in excl]
        deps = {}
        raw = set()
        for k in list(reads) + excl:
            w = self.last_w.get(k)
            if w is not None:
                deps[w.idx] = w
                raw.add(w.idx)
        for k in writes:
            w = self.last_w.get(k)
            if w is not None:
                deps[w.idx] = w
            for r in self.readers.get(k, ()):
                deps[r.idx] = r
        for k in reads:
            self.readers.setdefault(k, []).append(op)
        for k in writes:
            self.last_w[k] = op
            self.readers[k] = []
        if dma:
            slot = (eng, self.dma_rr[eng] % N_DMA_SEMS)
            self.dma_rr[eng] += 1
            prev = self.dma_last.get(slot)
            if prev is not None:
                deps[prev.idx] = prev
            self.dma_last[slot] = op
            op.sem = slot
            op.val = (prev.val if prev is not None else 0) + 16
        op.deps = list(deps.values())
        op.raw = raw
        self.ops.append(op)
        if fn is not None and not dma:
            self.last_eng[eng] = op
        return op

    def finalize_and_emit(self, sems):
        for op in self.ops:
            for d in op.deps:
                if d.dma:
                    continue
                if d.eng != op.eng or op.dma or (SAME_ENGINE_SYNC and d.eng != "pe"):
                    d.inc = True
        cnt = {e: 0 for e in self.ENGS}
        for op in self.ops:
            if not op.dma and op.inc:
                cnt[op.eng] += 1
                op.sem = op.eng
                op.val = cnt[op.eng]
        clock = {e: {} for e in self.ENGS}
        for op in self.ops:
            ck = clock[op.eng]
            waits = {}
            changed = False
            for d in sorted(op.deps, key=lambda o: -o.idx):
                if d.sem is None:
                    continue
                if (not d.dma) and d.eng == op.eng and not op.dma and not (SAME_ENGINE_SYNC and d.eng != "pe"):
                    continue
                if ck.get(d.sem, 0) >= d.val:
                    continue
                if waits.get(d.sem, 0) < d.val:
                    waits[d.sem] = d.val
                if not changed:
                    ck = dict(ck); changed = True
                for s, v in d.known.items():
                    if ck.get(s, 0) < v:
                        ck[s] = v
                if ck.get(d.sem, 0) < d.val:
                    ck[d.sem] = d.val
            op.waits = list(waits.items())
            if changed:
                clock[op.eng] = ck
            if op.sem is not None:
                if not op.dma:
                    ck2 = dict(ck); ck2[op.sem] = op.val
                    op.known = ck2
                    if op.eng == "pe" or not SAME_ENGINE_SYNC:
                        clock[op.eng] = ck2
                else:
                    op.known = ck
        per_eng = {e: [o for o in self.ops if o.eng == e] for e in self.ENGS}
        self.stats = {e: (len(per_eng[e]), sum(len(o.waits) for o in per_eng[e])) for e in self.ENGS}

        def emit(e, name):
            for op in per_eng[name]:
                for s, v in op.waits:
                    e.wait_ge(sems[s], v)
                if op.fn is not None:
                    ins = op.fn(e)
                    if op.sem is not None:
                        ins.then_inc(sems[op.sem], 16 if op.dma else 1)

        with self.nc.Block() as block:
            @block.sync
            def _(e):
                emit(e, "sp")

            @block.scalar
            def _(e):
                emit(e, "act")

            @block.vector
            def _(e):
                emit(e, "dve")

            @block.gpsimd
            def _(e):
                emit(e, "pool")

            @block.tensor
            def _(e):
                emit(e, "pe")


class Arena:
    def __init__(self, ap, words):
        self.ap = ap; self.words = words; self.off = 0; self.peak = 0

    def mark(self):
        return self.off

    def release(self, m):
        self.off = m

    def alloc(self, shape, dtype):
        p = shape[0]
        n = 1
        for s in shape[1:]:
            n *= s
        esz = 4 if dtype in (F32, I32) else 2
        words = (n * esz + 3) // 4
        words = (words + 7) // 8 * 8
        assert self.off + words <= self.words, ("arena overflow", self.off, words, self.words)
        sl = self.ap[:, self.off:self.off + words]
        self.off += words
        self.peak = max(self.peak, self.off)
        if dtype != F32:
            sl = sl.bitcast(dtype)
        sl = sl[0:p, 0:n]
        if len(shape) == 3:
            sl = sl.rearrange("p (a b) -> p a b", a=shape[1])
        elif len(shape) == 4:
            sl = sl.rearrange("p (a b c) -> p a b c", a=shape[1], b=shape[2])
        return sl


class Builder:
    def __init__(self, debug=None):
        self.debug = debug or {}
        self.nc = bass.Bass("TRN2", target_bir_lowering=False)
        self.P = Prog(self.nc)

    def dma(self, eng, out, in_, reads, writes, **kw):
        return self.P.add(eng, lambda e: e.dma_start(out=out, in_=in_, **kw), reads, writes, dma=True)

    def mm(self, out, lhsT, rhs, start, stop, reads, writes):
        return self.P.add("pe", lambda e: e.matmul(out, lhsT=lhsT, rhs=rhs, start=start, stop=stop), reads, writes)

    def pe_fence(self, reads, writes):
        z = self.zeros_b
        d = self.dummy_ps
        return self.P.add("pe", lambda e: e.matmul(d, lhsT=z[:, 0:1], rhs=z[:, 1:2], start=True, stop=True),
                          list(reads), list(writes) + ["bk3"])

    def tr(self, out, in_, ident, reads, writes):
        return self.P.add("pe", lambda e: e.transpose(out=out, in_=in_, identity=ident), reads, writes)

    def act(self, out, in_, func, reads, writes, bias=None, scale=None, accum_out=None):
        kw = {}
        if bias is not None:
            kw["bias"] = bias
        if scale is not None:
            kw["scale"] = scale
        if accum_out is not None:
            kw["accum_out"] = accum_out
        return self.P.add("act", lambda e: e.activation(out=out, in_=in_, func=func, **kw), reads, writes)

    def copy(self, eng, out, in_, reads, writes):
        if eng == "act":
            return self.act(out, in_, AF.Copy, reads, writes)
        return self.P.add(eng, lambda e: e.tensor_copy(out=out, in_=in_), reads, writes)

    def tt(self, eng, out, in0, in1, op, reads, writes):
        return self.P.add(eng, lambda e: e.tensor_tensor(out=out, in0=in0, in1=in1, op=op), reads, writes)

    def ts(self, eng, out, in0, s1, s2, op0, op1, reads, writes):
        if s2 is None:
            return self.P.add(eng, lambda e: e.tensor_scalar(out=out, in0=in0, scalar1=s1, scalar2=None, op0=op0), reads, writes)
        return self.P.add(eng, lambda e: e.tensor_scalar(out=out, in0=in0, scalar1=s1, scalar2=s2, op0=op0, op1=op1), reads, writes)

    def recip(self, out, in_, reads, writes):
        return self.P.add("dve", lambda e: e.reciprocal(out=out, in_=in_), reads, writes)

    def memset(self, eng, ap, val, reads, writes):
        return self.P.add(eng, lambda e: e.memset(ap, val), reads, writes)

    def asel(self, out, in_, pattern, cmp, fill, base, cm, reads, writes):
        return self.P.add("pool", lambda e: e.affine_select(out=out, in_=in_, pattern=pattern, compare_op=cmp,
                                                            fill=fill, base=base, channel_multiplier=cm), reads, writes)

    def vmax(self, out, in_, reads, writes):
        return self.P.add("dve", lambda e: e.max(out=out, in_=in_), reads, writes)

    def build(self):
        nc = self.nc
        dbg = self.debug
        stop_after = dbg.get("stop_after", None)
        n_layers = dbg.get("n_layers", DEPTH)

        self.declared = []
        SHAPES = {"x": ([S, D], F32), "c": ([D], F32), "pos": ([S], I32), "w_ada": ([DEPTH, D, 6 * D], F32),
                  "b_ada": ([DEPTH, 6 * D], F32), "norm1_g": ([DEPTH, D], F32), "norm2_g": ([DEPTH, D], F32),
                  "w_in": ([DEPTH, D, DIN], F32), "q_norm_g": ([DEPTH, 768], F32), "w_uq": ([DEPTH, 768, 768], F32),
                  "kv_norm_g": ([DEPTH, 256], F32), "w_ukv": ([DEPTH, 256, 1024], F32), "w_o_mla": ([DEPTH, 512, D], F32),
                  "w_o_sb": ([DEPTH, 512, D], F32), "w_o_moba": ([DEPTH, 512, D], F32), "w_out": ([DEPTH, D, D], F32),
                  "w_ff1": ([DEPTH, D, DFF], F32), "w_ff2": ([DEPTH, DFF, D], F32), "final_norm_g": ([D], F32)}
        bld = self

        class LazyIn(dict):
            def __missing__(self, name):
                shp, dt = SHAPES[name]
                ap = nc.dram_tensor(name, shp, dt, kind="ExternalInput").ap()
                bld.declared.append(name)
                self[name] = ap
                return ap

        IN = LazyIn()
        self.IN = IN
        x_in = IN["x"]; c_in = IN["c"]; pos_in = IN["pos"]
        w_ada = IN["w_ada"]; b_ada = IN["b_ada"]
        norm1_g = IN["norm1_g"]; norm2_g = IN["norm2_g"]; q_norm_g = IN["q_norm_g"]; kv_norm_g = IN["kv_norm_g"]
        fin_g = IN["final_norm_g"]
        y_out = nc.dram_tensor("y", [S, D], F32, kind="ExternalOutput").ap()

        dbg_scr = dbg.get("scratch_out", ())

        def scr(name, shape, dt=BF16):
            kind = "ExternalOutput" if name in dbg_scr else "Internal"
            return nc.dram_tensor(name, shape, dt, kind=kind).ap()

        modd = scr("modd", [DEPTH, 6 * D], F32)
        HT = scr("HT", [D, S])
        QA = scr("QA", [768, S]); KA = scr("KA", [512, S]); KPE = scr("KPE", [32, S]); VA = scr("VA", [S, 512])
        QS = scr("QS", [512, S]); KS = scr("KS", [512, S]); VS = scr("VS", [S, 512])
        QM = scr("QM", [512, S]); KM = scr("KM", [512, S]); VM = scr("VM", [S, 512]); MB = scr("MB", [128, S])
        OH = scr("OH", [16, S])
        OT = scr("OT", [3, 512, S])
        XM = scr("XM", [S, D], F32)
        XL = scr("XL", [S, D], F32)

        with ExitStack() as st:
            sems = {}
            for e in Prog.ENGS:
                sems[e] = st.enter_context(nc.semaphore("s_" + e))
            for e in ("sp", "pool", "act"):
                for i in range(N_DMA_SEMS):
                    sems[(e, i)] = st.enter_context(nc.semaphore("d_%s_%d" % (e, i)))
            AW = 50 * 1024
            arena_t = st.enter_context(nc.sbuf_tensor("arena", [128, AW], F32))
            A = Arena(arena_t[:], AW)
            banks = [st.enter_context(nc.psum_tensor("bank%d" % i, [128, 512], F32)) for i in range(8)]
            BK = [b[:] for b in banks]
            BKH = [b[:].bitcast(BF16) for b in banks]

            P = self.P
            B = self

            ident_b = A.alloc([128, 128], BF16)
            ident_f = A.alloc([128, 128], F32)
            ones_f = A.alloc([128, 1], F32)
            tri_neg = A.alloc([128, 128], BF16)
            neg_ones = A.alloc([128, 128], BF16)
            zeros_b = A.alloc([128, 64], BF16)
            cosA = A.alloc([128, NT, 16], F32); sinA = A.alloc([128, NT, 16], F32)
            cosB = A.alloc([128, NT, 8], F32); sinB = A.alloc([128, NT, 8], F32)
            modT = A.alloc([128, DEPTH, 48], F32)
            n1g = A.alloc([128, DEPTH, 8], F32); n2g = A.alloc([128, DEPTH, 8], F32)
            qng = A.alloc([128, DEPTH, 6], F32); kvng = A.alloc([128, DEPTH, 2], F32)
            A1 = A.alloc([128, DEPTH, 8], F32); A2 = A.alloc([128, DEPTH, 8], F32)
            gbc = A.alloc([128, D], F32)
            fing_bc = A.alloc([128, D], F32)
            persist_mark = A.mark()
            self.zeros_b = zeros_b
            self.dummy_ps = BK[3][0:1, 0:1]

            B.memset("pool", ident_f, 1.0, [], ["ident_f"])
            B.asel(ident_f, ident_f, [[-1, 128]], ALU.is_equal, 0.0, 0, 1, ["ident_f"], ["ident_f"])
            B.copy("dve", ident_b, ident_f, ["ident_f"], ["ident_b"])
            B.memset("pool", ones_f, 1.0, [], ["ones_f"])
            B.memset("pool", neg_ones, -1.0, [], ["neg_ones"])
            B.memset("pool", tri_neg, -1.0, [], ["tri_neg"])
            B.asel(tri_neg, tri_neg, [[-1, 128]], ALU.is_ge, 0.0, 0, 1, ["tri_neg"], ["tri_neg"])
            B.memset("pool", zeros_b, 0.0, [], ["zeros_b"])

            m0 = A.mark()
            pos_i = A.alloc([128, NT], I32); pos_f = A.alloc([128, NT], F32)
            invA = A.alloc([128, 16], F32); invB = A.alloc([128, 8], F32)
            B.dma("sp", pos_i, pos_in.rearrange("(j p) -> p j", p=128), [], ["pos_i"], allow_slow_non_contiguous=True)
            B.copy("dve", pos_f, pos_i, ["pos_i"], ["pos_f"])
            for i in range(16):
                B.memset("pool", invA[:, i:i + 1], float(np.float32(THETA) ** np.float32(-(2.0 * i) / 32.0)) / (2 * math.pi), [], ["invA"])
            for i in range(8):
                B.memset("pool", invB[:, i:i + 1], float(np.float32(THETA) ** np.float32(-(2.0 * i) / 16.0)) / (2 * math.pi), [], ["invB"])

            def trig_table(dst, inv, n, shift, key):
                y = A.alloc([128, NT, n], F32); yi = A.alloc([128, NT, n], I32); yf = A.alloc([128, NT, n], F32)
                w = A.alloc([128, NT, n], F32)
                for j in range(NT):
                    B.ts("dve", y[:, j, :], inv, pos_f[:, j:j + 1], shift, ALU.mult, ALU.add, ["pos_f", "invA", "invB"], [key + "y"])
                B.copy("dve", yi, y, [key + "y"], [key + "yi"])
                B.copy("dve", yf, yi, [key + "yi"], [key + "yf"])
                B.tt("dve", y, y, yf, ALU.subtract, [key + "y", key + "yf"], [key + "y"])
                B.ts("dve", w, y, 0.5, None, ALU.is_gt, None, [key + "y"], [key + "w"])
                B.tt("dve", y, y, w, ALU.subtract, [key + "y", key + "w"], [key + "y"])
                B.ts("dve", w, y, -0.5, None, ALU.is_lt, None, [key + "y"], [key + "w"])
                B.tt("dve", y, y, w, ALU.add, [key + "y", key + "w"], [key + "y"])
                B.act(dst, y, AF.Sin, [key + "y"], [key], scale=2 * math.pi)

            for (dst, inv, n, shift, key) in ((sinA, invA, 16, 0.0, "sinA"), (cosA, invA, 16, 0.25, "cosA"),
                                               (sinB, invB, 8, 0.0, "sinB"), (cosB, invB, 8, 0.25, "cosB")):
                trig_table(dst, inv, n, shift, key)

            oh = A.alloc([16, S], BF16)
            B.memset("pool", oh, 1.0, [], ["oh"])
            B.asel(oh, oh, [[1, S]], ALU.is_ge, 0.0, 0, -256, ["oh"], ["oh"])
            B.asel(oh, oh, [[-1, S]], ALU.is_gt, 0.0, 256, 256, ["oh"], ["oh"])
            B.dma("sp", OH, oh, ["oh"], ["OH"])

            if not dbg.get("no_mod", False):
                cT = A.alloc([128, 8], F32); cact = A.alloc([128, 8], F32)
                B.dma("sp", cT, c_in.rearrange("(j p) -> p j", p=128), [], ["cT"], allow_slow_non_contiguous=True)
                B.act(cact, cT, AF.Silu, ["cT"], ["cact"])
                modrow = A.alloc([1, 6 * D], F32); brow = A.alloc([1, 6 * D], F32)
                wa = [A.alloc([128, 8, 512], F32), A.alloc([128, 8, 512], F32)]
                it = 0
                for l in range(DEPTH):
                    B.dma("sp", brow, b_ada[l:l + 1, :], [], ["brow"])
                    for n in range(12):
                        wt = wa[it % 2]; wk = "wa%d" % (it % 2)
                        B.dma("sp" if it % 2 == 0 else "pool", wt, w_ada[l, :, n * 512:(n + 1) * 512].rearrange("(j p) c -> p j c", p=128), [], [wk])
                        bk = 1 + (it % 2)
                        B.pe_fence(["cact", wk, "zeros_b"], ["bk%d" % bk])
                        for j in range(8):
                            B.mm(BK[bk][0:1, :], cact[:, j:j + 1], wt[:, j, :], j == 0, j == 7, [], [])
                        B.pe_fence(["cact", wk, "zeros_b"], ["bk%d" % bk])
                        B.tt("dve", modrow[:, n * 512:(n + 1) * 512], BK[bk][0:1, :], brow[:, n * 512:(n + 1) * 512], ALU.add,
                             ["bk%d" % bk, "brow"], ["modrow"])
                        it += 1
                    B.dma("sp", modd[l:l + 1, :], modrow, ["modrow"], [("modd", l)])
                    B.dma("sp", modT[:, l, :], modd[l].rearrange("(c p) -> p c", p=128), [("modd", l)], [("modT", l)], allow_slow_non_contiguous=True)
                for (dst, src, nch, key) in ((n1g, norm1_g, 8, "n1g"), (n2g, norm2_g, 8, "n2g"), (qng, q_norm_g, 6, "qng"), (kvng, kv_norm_g, 2, "kvng")):
                    for l in range(DEPTH):
                        B.dma("sp", dst[:, l, :], src[l].rearrange("(c p) -> p c", p=128), [], [key], allow_slow_non_contiguous=True)
                B.dma("sp", fing_bc, fin_g.partition_broadcast(128), [], ["fing"])
                for l in range(DEPTH):
                    B.ts("dve", A1[:, l, :], modT[:, l, 8:16], 1.0, None, ALU.add, None, [("modT", l)], ["A1"])
                    B.tt("dve", A1[:, l, :], A1[:, l, :], n1g[:, l, :], ALU.mult, ["A1", "n1g"], ["A1"])
                    B.ts("dve", A2[:, l, :], modT[:, l, 32:40], 1.0, None, ALU.add, None, [("modT", l)], ["A2"])
                    B.tt("dve", A2[:, l, :], A2[:, l, :], n2g[:, l, :], ALU.mult, ["A2", "n2g"], ["A2"])
            A.release(m0)
            SETUP_KEYS = ["A1", "A2", "sinA", "cosA", "sinB", "cosB", "OH", "ident_b", "ident_f", "tri_neg", "fing",
                          "qng", "kvng", "ones_f", "neg_ones", "zeros_b", "wa0", "wa1", "modrow", "brow", "cact", "oh"]
            self.barrier(SETUP_KEYS + [("modT", l) for l in range(DEPTH)], "setup")

            if stop_after == "setup":
                self.finish(sems, [])
                return nc

            out_keys = []
            for l in range(n_layers):
                x_src = x_in if l == 0 else XL
                phases = dbg.get("phases", (1, 2, 3, 4))
                if 1 in phases:
                    self.phase1(l, A, BK, BKH, locals())
                if stop_after == ("p1", l):
                    break
                if 2 in phases:
                    self.phase2(l, A, BK, BKH, locals())
                if stop_after == ("p2", l):
                    break
                if 3 in phases:
                    self.phase3(l, A, BK, BKH, locals())
                if stop_after == ("p3", l):
                    break
                if 4 in phases:
                    self.phase4(l, A, BK, BKH, locals(), last=(l == DEPTH - 1))
                if stop_after == ("p4", l):
                    break
            self.finish(sems, [])
            self.arena_peak = A.peak
        return nc

    def barrier(self, keys=None, name=None):
        self.P.barrier()
        self.fence = []

    def finish(self, sems, keys):
        P = self.P
        last_dma_keys = []
        for slot, op in P.dma_last.items():
            k = ("lastdma", slot)
            P.last_w[k] = op
            last_dma_keys.append(k)
        P.add("sp", None, last_dma_keys, [])
        P.finalize_and_emit(sems)

    def phase1(self, l, A, BK, BKH, env):
        B = self; P = self.P
        g = env
        w_in = self.IN["w_in"]; w_uq = self.IN["w_uq"]; w_ukv = self.IN["w_ukv"]
        ident_b = g["ident_b"]; ident_f = g["ident_f"]; ones_f = g["ones_f"]
        cosA, sinA, cosB, sinB = g["cosA"], g["sinA"], g["cosB"], g["sinB"]
        modT, A1, qng, kvng = g["modT"], g["A1"], g["qng"], g["kvng"]
        x_src = g["x_src"]
        HT, QA, KA, KPE, VA, QS, KS, VS, QM, KM, VM, MB = (g[k] for k in ("HT", "QA", "KA", "KPE", "VA", "QS", "KS", "VS", "QM", "KM", "VM", "MB"))
        FZ = list(self.fence)
        mark = A.mark()
        L = "L%d" % l

        Win = A.alloc([128, 8, NPROJ], BF16)
        Wuq = A.alloc([128, 6, 768], BF16)
        Wukv = A.alloc([128, 2, 1024], BF16)
        for k in range(8):
            B.dma("pool", Win[:, k, :], w_in[l, k * 128:(k + 1) * 128, 0:NPROJ], FZ, [("Win", k)])
        for k in range(6):
            src = w_uq[l, k * 128:(k + 1) * 128, :].rearrange("p (h r) -> p h r", h=8)
            B.dma("pool", Wuq[:, k, 0:512].rearrange("p (h r) -> p h r", h=8), src[:, :, 0:64], FZ, [("Wuq", k, 0)])
            B.dma("pool", Wuq[:, k, 512:768].rearrange("p (h r) -> p h r", h=8), src[:, :, 64:96], FZ, [("Wuq", k, 1)])
        for k in range(2):
            src = w_ukv[l, k * 128:(k + 1) * 128, :].rearrange("p (h r) -> p h r", h=8)
            B.dma("pool", Wukv[:, k, 0:512].rearrange("p (h r) -> p h r", h=8), src[:, :, 0:64], FZ, [("Wukv", k, 0)])
            B.dma("pool", Wukv[:, k, 512:1024].rearrange("p (h r) -> p h r", h=8), src[:, :, 64:128], FZ, [("Wukv", k, 1)])
        WinK = [("Win", k) for k in range(8)]
        WuqK = [("Wuq", k, i) for k in range(6) for i in range(2)]
        WukvK = [("Wukv", k, i) for k in range(2) for i in range(2)]

        xt = [A.alloc([128, D], F32) for _ in range(2)]
        junk = A.alloc([128, D], F32)
        xn = A.alloc([128, D], BF16)
        hT = [A.alloc([128, 8, 512], BF16) for _ in range(2)]
        st_ = A.alloc([128, 8], F32)
        latn = A.alloc([128, 1024], BF16)
        latT = A.alloc([128, 8, 128], BF16)
        qtok = A.alloc([128, 8, 96], BF16)
        rt = [A.alloc([128, 8, 16], F32) for _ in range(4)]
        kn = A.alloc([128, 512], BF16)
        vb = [A.alloc([128, 512], BF16) for _ in range(3)]
        kpe_t = A.alloc([128, 32], BF16)
        qs_t = A.alloc([128, 512], BF16); ks_t = A.alloc([128, 512], BF16)
        qm_f = A.alloc([128, 512], F32); km_f = A.alloc([128, 512], F32)
        km_b = A.alloc([128, 512], BF16)
        QmT = A.alloc([128, 4, 128], BF16)
        qm_b = A.alloc([128, 512], BF16)
        ones_b = A.alloc([128, 1], BF16)
        kmhi = A.alloc([128, 4, 32], BF16); kmlo = A.alloc([128, 4, 32], BF16); ktmp = A.alloc([128, 4, 1], F32)
        kmsum = A.alloc([128, 4, 16], F32)
        G = A.alloc([128, 8, 16], F32); m8 = A.alloc([128, 8, 8], F32); sel = A.alloc([128, 8, 16], F32); mk2 = A.alloc([128, 8, 16], F32)
        mbt = A.alloc([128, 8, 16], BF16)
        sQA = A.alloc([96, 8, 512], BF16); sKA = A.alloc([128, 4, 512], BF16); sKPE = A.alloc([32, 512], BF16)
        sQS = A.alloc([128, 4, 512], BF16); sKS = A.alloc([128, 4, 512], BF16)
        sQM = A.alloc([128, 4, 512], BF16); sKM = A.alloc([128, 4, 512], BF16); sMB = A.alloc([128, 512], BF16)

        B.memset("pool", kmsum, 0.0, [], ["kmsum"])
        B.memset("pool", ones_b, 1.0, [], ["ones_b"])
        B.memset("pool", kmhi, 0.0, [], ["kmhl"])
        B.memset("pool", kmlo, 0.0, [], ["kmhl"])
        acc_rr = [1, 2, 5]
        acc_i = [0]

        def next_acc():
            b = acc_rr[acc_i[0] % 3]
            acc_i[0] += 1
            return b

        ev_i = [0]

        def ev_eng():
            ev_i[0] += 1
            return "act" if ev_i[0] % 2 == 0 else "dve"

        def rope(x1, x2, o1, o2, cs, sn, n, rkeys, wkeys, h=8):
            if h > 1:
                c_b = cs.unsqueeze(1).to_broadcast([128, h, n]); s_b = sn.unsqueeze(1).to_broadcast([128, h, n])
                t = [r[:, 0:h, 0:n] for r in rt]
            else:
                c_b = cs; s_b = sn
                t = [r[:, 0, 0:n] for r in rt]
            B.tt("dve", t[0], x1, c_b, ALU.mult, rkeys, ["rt0"])
            B.tt("dve", t[1], x2, s_b, ALU.mult, rkeys, ["rt1"])
            B.tt("dve", t[2], x1, s_b, ALU.mult, rkeys, ["rt2"])
            B.tt("dve", t[3], x2, c_b, ALU.mult, rkeys, ["rt3"])
            B.tt("pool", o1, t[0], t[1], ALU.subtract, ["rt0", "rt1"], wkeys)
            B.tt("pool", o2, t[2], t[3], ALU.add, ["rt2", "rt3"], wkeys)

        for tt_ in self.debug.get('p1_tile_list', range(self.debug.get('p1_tiles', NT))):
            gi = tt_ // 4; i = tt_ % 4
            xb = xt[tt_ % 2]; xk = "xt%d" % (tt_ % 2)
            hb = hT[gi % 2]; hk = ("hT", gi % 2)
            tok = slice(i * 128, (i + 1) * 128)
            B.dma("sp", xb, x_src[tt_ * 128:(tt_ + 1) * 128, :], FZ + [("XL", tt_)], [xk])
            B.act(junk, xb, AF.Square, [xk], ["junk", "ss"], accum_out=st_[:, 0:1])
            B.act(st_[:, 1:2], st_[:, 0:1], AF.Sqrt, ["ss"], ["sd"], scale=1.0 / D, bias=EPS)
            B.recip(st_[:, 2:3], st_[:, 1:2], ["sd"], ["rstd"])
            B.ts("pool", xn, xb, st_[:, 2:3], None, ALU.mult, None, [xk, "rstd"], ["xn"])
            if self.debug.get('p1_cut', 99) <= 1:
                continue
            for j in range(8):
                B.tr(BKH[0][:, j * 128:(j + 1) * 128], xn[:, j * 128:(j + 1) * 128], ident_b, ["xn", "ident_b"], ["bk0"])
            for j in range(8):
                if j % 2 == 0:
                    B.act(hb[:, j, tok], BKH[0][:, j * 128:(j + 1) * 128], AF.Identity, ["bk0", "A1", ("modT", l)], [hk + (i,)],
                          scale=A1[:, l, j:j + 1], bias=modT[:, l, j:j + 1])
                else:
                    B.ts("dve", hb[:, j, tok], BKH[0][:, j * 128:(j + 1) * 128], A1[:, l, j:j + 1], modT[:, l, j:j + 1], ALU.mult, ALU.add,
                         ["bk0", "A1", ("modT", l)], [hk + (i,)])
            if self.debug.get('p1_cut', 99) <= 2:
                continue
            hki = [hk + (i,)]

            def proj(c0, n):
                b = next_acc()
                for k in range(8):
                    B.mm(BK[b][:, 0:n], hb[:, k, tok], Win[:, k, c0:c0 + n], k == 0, k == 7, hki + [("Win", k)], ["bk%d" % b])
                return b

            b0 = proj(0, 512)
            b1 = proj(512, 512)
            B.act(junk[:, 0:512], BK[b0], AF.Square, ["bk%d" % b0], ["junk", "ssq0"], accum_out=st_[:, 3:4])
            B.act(junk[:, 0:256], BK[b1][:, 0:256], AF.Square, ["bk%d" % b1], ["junk", "ssq1"], accum_out=st_[:, 4:5])
            B.act(junk[:, 256:512], BK[b1][:, 256:512], AF.Square, ["bk%d" % b1], ["junk", "sskv"], accum_out=st_[:, 5:6])
            B.tt("dve", st_[:, 3:4], st_[:, 3:4], st_[:, 4:5], ALU.add, ["ssq0", "ssq1"], ["ssq0"])
            B.act(st_[:, 6:7], st_[:, 3:4], AF.Sqrt, ["ssq0"], ["rq"], scale=1.0 / 768, bias=EPS)
            B.act(st_[:, 7:8], st_[:, 5:6], AF.Sqrt, ["sskv"], ["rkv"], scale=1.0 / 256, bias=EPS)
            B.recip(st_[:, 6:8], st_[:, 6:8], ["rq", "rkv"], ["rq", "rkv"])
            B.act(latn[:, 0:512], BK[b0], AF.Identity, ["bk%d" % b0, "rq"], [("latn", 0)], scale=st_[:, 6:7])
            B.ts("dve", latn[:, 512:768], BK[b1][:, 0:256], st_[:, 6:7], None, ALU.mult, None, ["bk%d" % b1, "rq"], [("latn", 1)])
            B.ts("dve", latn[:, 768:1024], BK[b1][:, 256:512], st_[:, 7:8], None, ALU.mult, None, ["bk%d" % b1, "rkv"], [("latn", 2)])
            for j in range(8):
                B.tr(BKH[4][:, j * 128:(j + 1) * 128], latn[:, j * 128:(j + 1) * 128], ident_b,
                     [("latn", 0), ("latn", 1), ("latn", 2), "ident_b"], ["bk4"])
            for j in range(8):
                gsc = qng[:, l, j:j + 1] if j < 6 else kvng[:, l, j - 6:j - 5]
                if j % 2 == 0:
                    B.act(latT[:, j, :], BKH[4][:, j * 128:(j + 1) * 128], AF.Identity, ["bk4", "qng", "kvng"], [("latT", j)], scale=gsc)
                else:
                    B.ts("dve", latT[:, j, :], BKH[4][:, j * 128:(j + 1) * 128], gsc, None, ALU.mult, None, ["bk4", "qng", "kvng"], [("latT", j)])
            if self.debug.get('p1_cut', 99) <= 3:
                continue
            latK = [("latT", j) for j in range(8)]
            bq0 = next_acc()
            for k in range(6):
                B.mm(BK[bq0], latT[:, k, :], Wuq[:, k, 0:512], k == 0, k == 5, latK + WuqK, ["bk%d" % bq0])
            B.copy("dve", qtok[:, :, 0:64], BK[bq0].rearrange("p (h r) -> p h r", h=8), ["bk%d" % bq0], [("qtok", 0)])
            bq1 = next_acc()
            for k in range(6):
                B.mm(BK[bq1][:, 0:256], latT[:, k, :], Wuq[:, k, 512:768], k == 0, k == 5, latK + WuqK, ["bk%d" % bq1])
            pe_v = BK[bq1][:, 0:256].rearrange("p (h r) -> p h r", h=8)
            rope(pe_v[:, :, 0:16], pe_v[:, :, 16:32], qtok[:, :, 64:80], qtok[:, :, 80:96], cosA[:, tt_, :], sinA[:, tt_, :], 16,
                 ["bk%d" % bq1, "cosA", "sinA"], [("qtok", 1)])
            bk0_ = next_acc()
            for k in range(2):
                B.mm(BK[bk0_], latT[:, 6 + k, :], Wukv[:, k, 0:512], k == 0, k == 1, latK + WukvK, ["bk%d" % bk0_])
            B.copy("act", kn, BK[bk0_], ["bk%d" % bk0_], ["kn"])
            bv_ = next_acc()
            for k in range(2):
                B.mm(BK[bv_], latT[:, 6 + k, :], Wukv[:, k, 512:1024], k == 0, k == 1, latK + WukvK, ["bk%d" % bv_])
            B.copy("act", vb[0], BK[bv_], ["bk%d" % bv_], ["vb0"])
            B.dma("sp", VA[tt_ * 128:(tt_ + 1) * 128, :], vb[0], ["vb0"], [("VA", tt_)])
            for h in range(8):
                B.tr(BKH[6][0:96, h * 128:(h + 1) * 128], qtok[:, h, :], ident_b, [("qtok", 0), ("qtok", 1), "ident_b"], ["bk6"])
            B.copy(ev_eng(), sQA[:, :, tok], BKH[6][0:96, :].rearrange("p (h t) -> p h t", h=8), ["bk6"], [("sQA", i)])
            for pr in range(4):
                B.tr(BKH[6][:, pr * 128:(pr + 1) * 128], kn[:, pr * 128:(pr + 1) * 128], ident_b, ["kn", "ident_b"], ["bk6"])
            B.copy(ev_eng(), sKA[:, :, tok], BKH[6][:, 0:512].rearrange("p (h t) -> p h t", h=4), ["bk6"], [("sKA", i)])
            bp = next_acc()
            for k in range(8):
                B.mm(BK[bp][:, 0:32], hb[:, k, tok], Win[:, k, OFF_KPE:OFF_KPE + 32], k == 0, k == 7, hki + [("Win", k)], ["bk%d" % bp])
            rope(BK[bp][:, 0:16], BK[bp][:, 16:32], kpe_t[:, 0:16], kpe_t[:, 16:32], cosA[:, tt_, :], sinA[:, tt_, :], 16,
                 ["bk%d" % bp, "cosA", "sinA"], ["kpe_t"], h=1)
            B.tr(BKH[6][0:32, 0:128], kpe_t, ident_b, ["kpe_t", "ident_b"], ["bk6"])
            B.copy(ev_eng(), sKPE[:, tok], BKH[6][0:32, 0:128], ["bk6"], [("sKPE", i)])

            if self.debug.get('p1_cut', 99) <= 4:
                continue
            b = proj(OFF_SBQ, 512)
            B.act(qs_t, BK[b], AF.Copy, ["bk%d" % b], ["qs_t"], scale=0.125)
            for pr in range(4):
                B.tr(BKH[6][:, pr * 128:(pr + 1) * 128], qs_t[:, pr * 128:(pr + 1) * 128], ident_b, ["qs_t", "ident_b"], ["bk6"])
            B.copy(ev_eng(), sQS[:, :, tok], BKH[6][:, 0:512].rearrange("p (h t) -> p h t", h=4), ["bk6"], [("sQS", i)])
            b = proj(OFF_SBK, 512)
            B.copy("act", ks_t, BK[b], ["bk%d" % b], ["ks_t"])
            for pr in range(4):
                B.tr(BKH[6][:, pr * 128:(pr + 1) * 128], ks_t[:, pr * 128:(pr + 1) * 128], ident_b, ["ks_t", "ident_b"], ["bk6"])
            B.copy(ev_eng(), sKS[:, :, tok], BKH[6][:, 0:512].rearrange("p (h t) -> p h t", h=4), ["bk6"], [("sKS", i)])
            b = proj(OFF_SBV, 512)
            B.copy("act", vb[1], BK[b], ["bk%d" % b], ["vb1"])
            B.dma("sp", VS[tt_ * 128:(tt_ + 1) * 128, :], vb[1], ["vb1"], [("VS", tt_)])

            if self.debug.get('p1_cut', 99) <= 5:
                continue
            b = proj(OFF_MBK, 512)
            B.copy("act", km_f, BK[b], ["bk%d" % b], ["km_f"])
            kv3 = km_f.rearrange("p (h r) -> p h r", h=8)
            rope(kv3[:, :, 0:8], kv3[:, :, 8:16], kv3[:, :, 0:8], kv3[:, :, 8:16], cosB[:, tt_, :], sinB[:, tt_, :], 8,
                 ["km_f", "cosB", "sinB"], ["km_f"])
            B.copy("pool", km_b, km_f, ["km_f"], ["km_b"])
            for pr in range(4):
                B.tr(BKH[6][:, pr * 128:(pr + 1) * 128], km_b[:, pr * 128:(pr + 1) * 128], ident_b, ["km_b", "ident_b"], ["bk6"])
            B.copy(ev_eng(), sKM[:, :, tok], BKH[6][:, 0:512].rearrange("p (h t) -> p h t", h=4), ["bk6"], [("sKM", i)])
            nblk = tt_ // 2
            bcs = next_acc()
            for pr in range(4):
                B.mm(BK[bcs][:, pr:pr + 1], km_b[:, pr * 128:(pr + 1) * 128], ones_b, True, True, ["km_b", "ones_b"], ["bk%d" % bcs])
            if tt_ % 2 == 0:
                B.copy("dve", kmsum[:, :, nblk], BK[bcs][:, 0:4], ["bk%d" % bcs], ["kmsum"])
            else:
                B.tt("dve", kmsum[:, :, nblk], kmsum[:, :, nblk], BK[bcs][:, 0:4], ALU.add, ["bk%d" % bcs, "kmsum"], ["kmsum"])
                for (r0_, coff) in ((0, 0), (64, 16)):
                    rs_ = slice(r0_, r0_ + 64)
                    B.copy("dve", kmhi[rs_, :, coff + nblk], kmsum[rs_, :, nblk], ["kmsum"], ["kmhl"])
                    B.tt("dve", ktmp[rs_, :, 0], kmsum[rs_, :, nblk], kmhi[rs_, :, coff + nblk], ALU.subtract, ["kmsum", "kmhl"], ["ktmp"])
                    B.copy("dve", kmlo[rs_, :, coff + nblk], ktmp[rs_, :, 0], ["ktmp"], ["kmhl"])

            if self.debug.get('p1_cut', 99) <= 6:
                continue
            b = proj(OFF_MBQ, 512)
            B.copy("act", qm_f, BK[b], ["bk%d" % b], ["qm_f"])
            qv3 = qm_f.rearrange("p (h r) -> p h r", h=8)
            rope(qv3[:, :, 0:8], qv3[:, :, 8:16], qv3[:, :, 0:8], qv3[:, :, 8:16], cosB[:, tt_, :], sinB[:, tt_, :], 8,
                 ["qm_f", "cosB", "sinB"], ["qm_f"])
            B.copy("pool", qm_b, qm_f, ["qm_f"], ["qm_b"])
            for pr in range(4):
                B.tr(BKH[6][:, pr * 128:(pr + 1) * 128], qm_b[:, pr * 128:(pr + 1) * 128], ident_b, ["qm_b", "ident_b"], ["bk6"])
            B.copy("dve", QmT, BKH[6][:, 0:512].rearrange("p (h t) -> p h t", h=4), ["bk6"], ["QmT"])
            B.copy("dve", sQM[:, :, tok], BKH[6][:, 0:512].rearrange("p (h t) -> p h t", h=4), ["bk6"], [("sQM", i)])
            b = proj(OFF_MBV, 512)
            B.copy("act", vb[2], BK[b], ["bk%d" % b], ["vb2"])
            B.dma("sp", VM[tt_ * 128:(tt_ + 1) * 128, :], vb[2], ["vb2"], [("VM", tt_)])

            if self.debug.get('p1_cut', 99) <= 7:
                continue
            cur = tt_ // 2
            gcut = self.debug.get("gate_cut", 99)
            if cur >= 4:
                bg = next_acc()
                for pr in range(4):
                    B.mm(BK[bg][:, pr * 32:(pr + 1) * 32], QmT[:, pr, :], kmhi[:, pr, :], True, False, ["QmT", "kmhl"], ["bk%d" % bg])
                    B.mm(BK[bg][:, pr * 32:(pr + 1) * 32], QmT[:, pr, :], kmlo[:, pr, :], False, True, ["QmT", "kmhl"], ["bk%d" % bg])
                if gcut >= 2:
                    B.memset("pool", G, -1e30, [], ["G"])
                    B.copy("dve", G[:, :, 0:cur], BK[bg][:, 0:128].rearrange("p (h n) -> p h n", h=8)[:, :, 0:cur], ["bk%d" % bg], ["G"])
                if self.debug.get("dump_G") == tt_:
                    if "GD" not in self.__dict__:
                        self.GD = self.nc.dram_tensor("GDBG", [128, 128], F32, kind="ExternalOutput").ap()
                        self.GD2 = self.nc.dram_tensor("GDBG2", [128, 64], F32, kind="ExternalOutput").ap()
                        self.GD3 = self.nc.dram_tensor("GDBG3", [128, 512], F32, kind="ExternalOutput").ap()
                    B.dma("sp", self.GD, G.rearrange("p h n -> p (h n)"), ["G"], ["GD"])
                    B.dma("sp", self.GD2, kmsum.rearrange("p h n -> p (h n)"), ["kmsum"], ["GD2"])
                    B.dma("sp", self.GD3, QmT.rearrange("p h n -> p (h n)"), ["QmT"], ["GD3"])
                if gcut >= 3:
                    G2 = sel
                    B.P.add("dve", lambda e: e.reduce_max(out=m8[:, :, 0], in_=G, axis=AX.X), ["G"], ["m8"])
                    B.tt("dve", G2, G, m8[:, :, 0:1].to_broadcast([128, 8, 16]), ALU.is_ge, ["G", "m8"], ["sel"])
                    B.P.add("dve", lambda e: e.scalar_tensor_tensor(out=G2, in0=G2, scalar=-3e30, in1=G, op0=ALU.mult, op1=ALU.add), ["sel", "G"], ["sel"])
                    B.P.add("dve", lambda e: e.reduce_max(out=m8[:, :, 1], in_=G2, axis=AX.X), ["sel"], ["m8"])
                    B.tt("dve", mk2, G2, m8[:, :, 1:2].to_broadcast([128, 8, 16]), ALU.is_ge, ["sel", "m8"], ["mk2"])
                    B.P.add("dve", lambda e: e.scalar_tensor_tensor(out=G2, in0=mk2, scalar=-3e30, in1=G2, op0=ALU.mult, op1=ALU.add), ["mk2", "sel"], ["sel"])
                    B.P.add("dve", lambda e: e.reduce_max(out=m8[:, :, 2], in_=G2, axis=AX.X), ["sel"], ["m8"])
                if gcut >= 4:
                    B.tt("dve", sel, G, m8[:, :, 2:3].to_broadcast([128, 8, 16]), ALU.is_ge, ["G", "m8"], ["sel"])
                if gcut >= 5:
                    B.ts("dve", mbt, sel, -MASKNEG, MASKNEG, ALU.mult, ALU.add, ["sel"], ["mbt"])
                B.memset("pool", mbt[:, :, cur:cur + 1], 0.0, [], ["mbt"])
            else:
                B.memset("pool", mbt, MASKNEG, [], ["mbt"])
                B.memset("pool", mbt[:, :, 0:cur + 1], 0.0, [], ["mbt"])
            B.tr(BKH[6][:, 0:128], mbt.rearrange("p h n -> p (h n)"), ident_b, ["mbt", "ident_b"], ["bk6"])
            B.copy(ev_eng(), sMB[:, tok], BKH[6][:, 0:128], ["bk6"], [("sMB", i)])

            if i == 3:
                cs = slice(gi * 512, (gi + 1) * 512)
                B.dma("sp", HT[:, cs].rearrange("(j p) t -> p j t", p=128), hb, [hk + (ii,) for ii in range(4)], [("HT", gi)])
                B.dma("sp", QA[:, cs].rearrange("(h r) t -> r h t", r=96), sQA, [("sQA", ii) for ii in range(4)], [("QA", gi)])
                B.dma("sp", KA[:, cs].rearrange("(c p) t -> p c t", p=128), sKA, [("sKA", ii) for ii in range(4)], [("KA", gi)])
                B.dma("sp", KPE[:, cs], sKPE, [("sKPE", ii) for ii in range(4)], [("KPE", gi)])
                B.dma("sp", QS[:, cs].rearrange("(c p) t -> p c t", p=128), sQS, [("sQS", ii) for ii in range(4)], [("QS", gi)])
                B.dma("sp", KS[:, cs].rearrange("(c p) t -> p c t", p=128), sKS, [("sKS", ii) for ii in range(4)], [("KS", gi)])
                B.dma("sp", QM[:, cs].rearrange("(c p) t -> p c t", p=128), sQM, [("sQM", ii) for ii in range(4)], [("QM", gi)])
                B.dma("sp", KM[:, cs].rearrange("(c p) t -> p c t", p=128), sKM, [("sKM", ii) for ii in range(4)], [("KM", gi)])
                B.dma("sp", MB[:, cs], sMB, [("sMB", ii) for ii in range(4)], [("MB", gi)])

        keys = [("hT", 0, ii) for ii in range(4)] + [("hT", 1, ii) for ii in range(4)]
        keys += ["xt0", "xt1", "junk", "xn", "kn", "vb0", "vb1", "vb2", "kpe_t", "qs_t", "ks_t", "qm_f", "km_f", "km_b", "QmT", "kmsum", "G", "m8",
                 "sel", "mbt", "rt0", "rt1", "rt2", "rt3", ("qtok", 0), ("qtok", 1), ("latn", 0), ("latn", 1), ("latn", 2)]
        keys += [("latT", j) for j in range(8)] + WinK + WuqK + WukvK
        for nm in ("sQA", "sKA", "sKPE", "sQS", "sKS", "sQM", "sKM", "sMB"):
            keys += [(nm, ii) for ii in range(4)]
        keys += ["bk%d" % b for b in range(8)]
        self.barrier(keys, "p1_%d" % l)
        A.release(mark)

    def phase2(self, l, A, BK, BKH, env):
        B = self; P = self.P
        g = env
        tri_neg, neg_ones, zeros_b = g["tri_neg"], g["neg_ones"], g["zeros_b"]
        QA, KA, KPE, VA, QS, KS, VS, QM, KM, VM, MB, OH, OT = (g[k] for k in ("QA", "KA", "KPE", "VA", "QS", "KS", "VS", "QM", "KM", "VM", "MB", "OH", "OT"))
        mark = A.mark()
        QT = [A.alloc([96, S], BF16) for _ in range(2)]
        KT = [A.alloc([96, S], BF16) for _ in range(2)]
        V = [A.alloc([128, NT, 128], BF16) for _ in range(2)]
        Pt = [A.alloc([128, 512], BF16) for _ in range(4)]
        U = [A.alloc([128, 512], F32) for _ in range(3)]
        Lb = [A.alloc([128, 512], BF16) for _ in range(3)]
        R = A.alloc([128, 512], BF16)
        rd = A.alloc([64, 512], F32)
        ost = [A.alloc([64, S], BF16) for _ in range(2)]
        for p in range(2):
            B.memset("pool", V[p][:, :, 64:128], 1.0, [], [("Vones", p)])
        allg = list(range(NG)); allt = list(range(NT))
        units = [(br, h) for br in self.debug.get("p2_branches", (0, 1, 2)) for h in range(self.debug.get("p2_heads", 8))]
        cnt = {"s": 0, "p": 0, "e": 0, "u": 0}
        def unit_params(u):
            br, h = units[u]
            par = u % 2
            qt, kt_, vv, os_ = QT[par], KT[par], V[par], ost[par]
            qk = [("QT", par, 0), ("QT", par, 1)]; kk = [("KT", par, 0), ("KT", par, 1)]; vk = [("V", par), ("Vones", par)]
            dk, scale = ((96, 1.0 / math.sqrt(96.0)), (64, 1.0), (80, 0.125))[br]
            return br, h, par, qt, kt_, vv, os_, qk, kk, vk, dk, scale

        def unit_loads(u):
            br, h, par, qt, kt_, vv, os_, qk, kk, vk, dk, scale = unit_params(u)
            if br == 0:
                B.dma("pool", qt[0:96, :], QA[h * 96:(h + 1) * 96, :], [("QA", gg) for gg in allg], qk)
                B.dma("pool", kt_[0:64, :], KA[h * 64:(h + 1) * 64, :], [("KA", gg) for gg in allg], [kk[0]])
                B.dma("pool", kt_[64:96, :], KPE[:, :], [("KPE", gg) for gg in allg], [kk[1]])
                B.dma("pool", vv[:, :, 0:64], VA[:, h * 64:(h + 1) * 64].rearrange("(j p) d -> p j d", p=128), [("VA", t) for t in allt], [vk[0]])
            elif br == 1:
                B.dma("pool", qt[0:64, :], QS[h * 64:(h + 1) * 64, :], [("QS", gg) for gg in allg], qk)
                B.dma("pool", kt_[0:64, :], KS[h * 64:(h + 1) * 64, :], [("KS", gg) for gg in allg], kk)
                B.dma("pool", vv[:, :, 0:64], VS[:, h * 64:(h + 1) * 64].rearrange("(j p) d -> p j d", p=128), [("VS", t) for t in allt], [vk[0]])
            else:
                B.dma("pool", qt[0:64, :], QM[h * 64:(h + 1) * 64, :], [("QM", gg) for gg in allg], [qk[0]])
                B.dma("pool", qt[64:80, :], MB[h * 16:(h + 1) * 16, :], [("MB", gg) for gg in allg], [qk[1]])
                B.dma("pool", kt_[0:64, :], KM[h * 64:(h + 1) * 64, :], [("KM", gg) for gg in allg], [kk[0]])
                B.dma("pool", kt_[64:80, :], OH[:, :], ["OH"], [kk[1]])
                B.dma("pool", vv[:, :, 0:64], VM[:, h * 64:(h + 1) * 64].rearrange("(j p) d -> p j d", p=128), [("VM", t) for t in allt], [vk[0]])

        if units:
            unit_loads(0)
        for u in range(len(units)):
            br, h, par, qt, kt_, vv, os_, qk, kk, vk, dk, scale = unit_params(u)
            if u + 1 < len(units):
                unit_loads(u + 1)
            for gq in range(self.debug.get("p2_groups", NG)):
                ob = 5 + (gq % 2); obk = "bk%d" % ob
                qs0 = gq * 512
                if br != 1:
                    nkt = 4 * gq + 4
                    tiles = []
                    for kt in range(nkt):
                        i = kt - 4 * gq
                        c0 = 128 * i if i > 0 else 0
                        tiles.append((kt, i, c0))

                    def stA(t):
                        kt, i, c0 = t
                        sbk = cnt["s"] % 5; cnt["s"] += 1
                        pi = cnt["p"] % 4; cnt["p"] += 1
                        pb = Pt[pi]; pk = ("Pt", pi)
                        B.mm(BK[sbk][:, c0:512], kt_[0:dk, kt * 128:(kt + 1) * 128], qt[0:dk, qs0 + c0:qs0 + 512], True, True, qk + kk, ["bk%d" % sbk])
                        B.act(pb[:, c0:512], BK[sbk][:, c0:512], AF.Exp, ["bk%d" % sbk], [pk], scale=scale)
                        if i >= 0:
                            B.asel(pb[:, c0:c0 + 128], pb[:, c0:c0 + 128], [[1, 128]], ALU.is_ge, 0.0, 0, -1, [pk], [pk])
                        return pb, pk

                    def stE(t, pbk):
                        kt, i, c0 = t
                        pb, pk = pbk
                        B.mm(BK[ob][:, c0:512], vv[:, kt, :], pb[:, c0:512], kt == 0, kt == nkt - 1, vk + [pk], [obk])

                    prev = stA(tiles[0])
                    for si in range(nkt):
                        nxt = stA(tiles[si + 1]) if si + 1 < nkt else None
                        stE(tiles[si], prev)
                        prev = nxt
                    B.recip(rd[0:64, :], BK[ob][64:128, :], [obk], ["rd"])
                    B.tt("dve", os_[0:64, qs0:qs0 + 512], BK[ob][0:64, :], rd[0:64, :], ALU.mult, [obk, "rd"], [("ost", par)])
                else:
                    B.memset("pool", R, 0.0, [], ["R"])
                    B.mm(BK[ob][0:64, :], zeros_b[:, 0:64], R, True, False, ["zeros_b", "R"], [obk])
                    tiles = []
                    for kt in range(4 * gq + 3, -1, -1):
                        i = kt - 4 * gq
                        c0 = 128 * i if i > 0 else 0
                        tiles.append((kt, i, c0))
                    n_t = len(tiles)

                    def sbA(t):
                        kt, i, c0 = t
                        zbk = cnt["s"] % 3; cnt["s"] += 1
                        ui = cnt["u"] % 3; cnt["u"] += 1
                        ksl = kt_[0:64, kt * 128:(kt + 1) * 128]; qsl = qt[0:64, qs0 + c0:qs0 + 512]
                        B.mm(BK[zbk][:, c0:512], ksl, qsl, True, True, qk + kk, ["bk%d" % zbk])
                        B.act(U[ui][:, c0:512], BK[zbk][:, c0:512], AF.Exp, ["bk%d" % zbk], [("U", ui)])
                        B.act(Lb[ui][:, c0:512], U[ui][:, c0:512], AF.Ln, [("U", ui)], [("Lb", ui)], bias=1.0)
                        if i >= 0:
                            B.asel(Lb[ui][:, c0:c0 + 128], Lb[ui][:, c0:c0 + 128], [[1, 128]], ALU.is_gt, 0.0, 0, -1, [("Lb", ui)], [("Lb", ui)])
                        return ui

                    def sbC(t, ui, first):
                        kt, i, c0 = t
                        ebk = 3 + cnt["e"] % 2; cnt["e"] += 1
                        pi = cnt["p"] % 4; cnt["p"] += 1
                        pb = Pt[pi]; pk = ("Pt", pi)
                        ksl = kt_[0:64, kt * 128:(kt + 1) * 128]; qsl = qt[0:64, qs0 + c0:qs0 + 512]
                        B.mm(BK[ebk][:, c0:512], ksl, qsl, True, False, qk + kk, ["bk%d" % ebk])
                        B.mm(BK[ebk][:, c0:512], tri_neg, Lb[ui][:, c0:512], False, first, ["tri_neg", ("Lb", ui)], ["bk%d" % ebk])
                        if not first:
                            B.mm(BK[ebk][:, c0:512], neg_ones, R[:, c0:512], False, True, ["neg_ones", "R"], ["bk%d" % ebk])
                        B.act(pb[:, c0:512], BK[ebk][:, c0:512], AF.Exp, ["bk%d" % ebk], [pk])
                        if i >= 0:
                            B.asel(pb[:, c0:c0 + 128], pb[:, c0:c0 + 128], [[1, 128]], ALU.is_gt, 0.0, 0, -1, [pk], [pk])
                        if kt > 0:
                            B.tt("dve", R[:, c0:512], R[:, c0:512], Lb[ui][:, c0:512], ALU.add, ["R", ("Lb", ui)], ["R"])
                        return pb, pk

                    def sbE(t, pbk):
                        kt, i, c0 = t
                        pb, pk = pbk
                        B.mm(BK[ob][0:64, c0:512], vv[:, kt, 0:64], pb[:, c0:512], False, kt == 0, vk + [pk], [obk])

                    uis = [None] * n_t
                    pbs = [None] * n_t
                    uis[0] = sbA(tiles[0])
                    for si in range(n_t + 1):
                        if si + 1 < n_t:
                            uis[si + 1] = sbA(tiles[si + 1])
                        if si < n_t:
                            pbs[si] = sbC(tiles[si], uis[si], si == 0)
                        if si >= 1:
                            sbE(tiles[si - 1], pbs[si - 1])
                    B.copy("dve", os_[0:64, qs0:qs0 + 512], BK[ob][0:64, :], [obk], [("ost", par)])
            B.dma("sp", OT[br, h * 64:(h + 1) * 64, :], os_[0:64, :], [("ost", par)], [("OT", br, h)])
        self.barrier()
        A.release(mark)

    def phase3(self, l, A, BK, BKH, env):
        B = self; P = self.P
        g = env
        IN = self.IN
        w_in = IN["w_in"]; w_out = IN["w_out"]
        w_o = [IN["w_o_mla"], IN["w_o_sb"], IN["w_o_moba"]]
        HT, OT, XM, modd, gbc = g["HT"], g["OT"], g["XM"], g["modd"], g["gbc"]
        x_src = g["x_src"]
        mark = A.mark()
        Wg = A.alloc([128, 8, 3072], BF16)
        Wo = [A.alloc([128, 4, D], BF16) for _ in range(3)]
        Wout = A.alloc([128, 8, D], BF16)
        for k in range(8):
            B.dma("pool", Wg[:, k, :], w_in[l, k * 128:(k + 1) * 128, OFF_G:DIN], [], [("Wg", k)])
            B.dma("pool", Wout[:, k, :], w_out[l, k * 128:(k + 1) * 128, :], [], [("Wout", k)])
        for br in range(3):
            for k in range(4):
                B.dma("pool", Wo[br][:, k, :], w_o[br][l, k * 128:(k + 1) * 128, :], [], [("Wo", br, k)])
        B.dma("sp", gbc, modd[l, 16 * 128:24 * 128].partition_broadcast(128), [("modd", l)], ["gbc"])
        WgK = [("Wg", k) for k in range(8)]; WoutK = [("Wout", k) for k in range(8)]
        hT = [A.alloc([128, 8, 512], BF16) for _ in range(2)]
        oT = [[A.alloc([128, 4, 512], BF16) for _ in range(3)] for _ in range(2)]
        sig = [A.alloc([128, 512], F32) for _ in range(2)]
        prod = [A.alloc([128, 512], F32) for _ in range(2)]
        macc = A.alloc([128, 512], F32)
        mT = A.alloc([128, 8, 512], BF16)
        xt = [A.alloc([128, D], F32) for _ in range(2)]
        xo = [A.alloc([128, D], F32) for _ in range(2)]
        tmp = [A.alloc([128, 512], F32) for _ in range(2)]
        c = {"g": 0, "b": 0, "s": 0, "o": 0, "t": 0}
        for gi in range(NG):
            par = gi % 2
            cs = slice(gi * 512, (gi + 1) * 512)
            B.dma("sp", hT[par], HT[:, cs].rearrange("(j p) t -> p j t", p=128), [("HT", gi)], [("hT3", par)])
            for br in range(3):
                B.dma("sp", oT[par][br], OT[br][:, cs].rearrange("(k p) t -> p k t", p=128), [("OT", br, h) for h in range(8)], [("oT3", par, br)])
            for j in range(8):
                for br in range(3):
                    gb = c["g"] % 3; c["g"] += 1
                    bb = 3 + c["b"] % 3; c["b"] += 1
                    si = c["s"] % 2; c["s"] += 1
                    for k in range(8):
                        B.mm(BK[gb], Wg[:, k, br * D + j * 128:br * D + (j + 1) * 128], hT[par][:, k, :], k == 0, k == 7,
                             [("Wg", k), ("hT3", par)], ["bk%d" % gb])
                    for k in range(4):
                        B.mm(BK[bb], Wo[br][:, k, j * 128:(j + 1) * 128], oT[par][br][:, k, :], k == 0, k == 3,
                             [("Wo", br, k), ("oT3", par, br)], ["bk%d" % bb])
                    B.act(sig[si], BK[gb], AF.Sigmoid, ["bk%d" % gb], [("sig", si)])
                    if br == 0:
                        B.tt("dve", macc, BK[bb], sig[si], ALU.mult, ["bk%d" % bb, ("sig", si)], ["macc"])
                    else:
                        B.tt("dve", prod[si], BK[bb], sig[si], ALU.mult, ["bk%d" % bb, ("sig", si)], [("prod", si)])
                        if br == 1:
                            B.tt("pool", macc, macc, prod[si], ALU.add, ["macc", ("prod", si)], ["macc"])
                        else:
                            B.tt("pool", mT[:, j, :], macc, prod[si], ALU.add, ["macc", ("prod", si)], [("mT", j)])
            for i in range(4):
                tt_ = gi * 4 + i
                xb = xt[tt_ % 2]; xk = ("xt3", tt_ % 2)
                yb = xo[tt_ % 2]; yk = ("xo3", tt_ % 2)
                B.dma("sp", xb, x_src[tt_ * 128:(tt_ + 1) * 128, :], [("XL", tt_)], [xk])
                for half in range(2):
                    ob = 6 + c["o"] % 2; c["o"] += 1
                    ti = c["t"] % 2; c["t"] += 1
                    hs = slice(half * 512, (half + 1) * 512)
                    for k in range(8):
                        B.mm(BK[ob], mT[:, k, i * 128:(i + 1) * 128], Wout[:, k, hs], k == 0, k == 7, [("mT", k), ("Wout", k)], ["bk%d" % ob])
                    B.tt("dve", tmp[ti], BK[ob], gbc[:, hs], ALU.mult, ["bk%d" % ob, "gbc"], [("tmp3", ti)])
                    B.tt("pool", yb[:, hs], xb[:, hs], tmp[ti], ALU.add, [xk, ("tmp3", ti)], [yk + (half,)])
                B.dma("sp", XM[tt_ * 128:(tt_ + 1) * 128, :], yb, [yk + (0,), yk + (1,)], [("XM", tt_)])
        self.barrier()
        A.release(mark)

    def phase4(self, l, A, BK, BKH, env, last):
        B = self; P = self.P
        g = env
        IN = self.IN
        w_ff1 = IN["w_ff1"]; w_ff2 = IN["w_ff2"]
        XM, XL, modd, gbc, fing_bc, y_out = g["XM"], g["XL"], g["modd"], g["gbc"], g["fing_bc"], g["y_out"]
        ident_b, modT, A2 = g["ident_b"], g["modT"], g["A2"]
        mark = A.mark()
        Wf1 = A.alloc([128, 8, DFF], BF16)
        Wf2 = A.alloc([128, 32, D], BF16)
        for k in range(8):
            B.dma("pool", Wf1[:, k, :], w_ff1[l, k * 128:(k + 1) * 128, :], [], [("Wf1", k)])
        for k in range(32):
            B.dma("pool", Wf2[:, k, :], w_ff2[l, k * 128:(k + 1) * 128, :], [], [("Wf2", k)])
        B.dma("sp", gbc, modd[l, 40 * 128:48 * 128].partition_broadcast(128), [("modd", l)], ["gbc"])
        GT = 256
        xt = [A.alloc([128, D], F32) for _ in range(2)]
        xn = A.alloc([128, D], BF16)
        st_ = A.alloc([128, 8], F32)
        h2T = A.alloc([128, 8, GT], BF16)
        hff = A.alloc([128, 32, GT], BF16)
        rl = [A.alloc([128, GT], F32) for _ in range(2)]
        xo = [A.alloc([128, D], F32) for _ in range(2)]
        tmp = [A.alloc([128, 512], F32) for _ in range(2)]
        c = {"a": 0, "r": 0, "o": 0, "t": 0}
        for gi in range(S // GT):
            for i in range(GT // 128):
                tt_ = gi * (GT // 128) + i
                xb = xt[i]; xk = ("xt4", i)
                tok = slice(i * 128, (i + 1) * 128)
                B.dma("sp", xb, XM[tt_ * 128:(tt_ + 1) * 128, :], [("XM", tt_)], [xk])
                B.act(xo[i], xb, AF.Square, [xk], [("xo4", i, 0), ("xo4", i, 1), "ss4"], accum_out=st_[:, 0:1])
                B.act(st_[:, 1:2], st_[:, 0:1], AF.Sqrt, ["ss4"], ["sd4"], scale=1.0 / D, bias=EPS)
                B.recip(st_[:, 2:3], st_[:, 1:2], ["sd4"], ["rstd4"])
                B.ts("pool", xn, xb, st_[:, 2:3], None, ALU.mult, None, [xk, "rstd4"], ["xn4"])
                for j in range(8):
                    B.tr(BKH[0][:, j * 128:(j + 1) * 128], xn[:, j * 128:(j + 1) * 128], ident_b, ["xn4", "ident_b"], ["bk0"])
                for j in range(8):
                    if j % 2 == 0:
                        B.act(h2T[:, j, tok], BKH[0][:, j * 128:(j + 1) * 128], AF.Identity, ["bk0", "A2", ("modT", l)], [("h2T", i)],
                              scale=A2[:, l, j:j + 1], bias=modT[:, l, 24 + j:25 + j])
                    else:
                        B.ts("dve", h2T[:, j, tok], BKH[0][:, j * 128:(j + 1) * 128], A2[:, l, j:j + 1], modT[:, l, 24 + j:25 + j], ALU.mult, ALU.add,
                             ["bk0", "A2", ("modT", l)], [("h2T", i)])
            hk = [("h2T", i) for i in range(GT // 128)]
            for cc in range(32):
                ab = 1 + c["a"] % 4; c["a"] += 1
                ri = c["r"] % 2; c["r"] += 1
                for k in range(8):
                    B.mm(BK[ab][:, 0:GT], Wf1[:, k, cc * 128:(cc + 1) * 128], h2T[:, k, :], k == 0, k == 7, [("Wf1", k)] + hk, ["bk%d" % ab])
                B.act(rl[ri], BK[ab][:, 0:GT], AF.Relu, ["bk%d" % ab], [("rl", ri)])
                B.tt("pool" if cc % 2 == 0 else "dve", hff[:, cc, :], rl[ri], rl[ri], ALU.mult, [("rl", ri)], [("hff", cc)])
            hfk = [("hff", cc) for cc in range(32)]
            for i in range(GT // 128):
                tt_ = gi * (GT // 128) + i
                xb = xt[i]; xk = ("xt4", i)
                yb = xo[i]
                for half in range(2):
                    ob = 5 + c["o"] % 3; c["o"] += 1
                    ti = c["t"] % 2; c["t"] += 1
                    hs = slice(half * 512, (half + 1) * 512)
                    for k in range(32):
                        B.mm(BK[ob], hff[:, k, i * 128:(i + 1) * 128], Wf2[:, k, hs], k == 0, k == 31, hfk + [("Wf2", k)], ["bk%d" % ob])
                    B.tt("dve", tmp[ti], BK[ob], gbc[:, hs], ALU.mult, ["bk%d" % ob, "gbc"], [("tmp4", ti)])
                    B.tt("pool", yb[:, hs], xb[:, hs], tmp[ti], ALU.add, [xk, ("tmp4", ti)], [("xo4", i, half)])
                yk = [("xo4", i, 0), ("xo4", i, 1)]
                if not last:
                    B.dma("sp", XL[tt_ * 128:(tt_ + 1) * 128, :], yb, yk, [("XL", tt_)])
                else:
                    B.act(xn.bitcast(F32)[:, 0:512] if False else tmp[0], yb[:, 0:512], AF.Square, yk, [("tmp4", 0), "fs0"], accum_out=st_[:, 3:4])
                    B.act(tmp[1], yb[:, 512:1024], AF.Square, yk, [("tmp4", 1), "fs1"], accum_out=st_[:, 4:5])
                    B.tt("dve", st_[:, 3:4], st_[:, 3:4], st_[:, 4:5], ALU.add, ["fs0", "fs1"], ["fs0"])
                    B.act(st_[:, 5:6], st_[:, 3:4], AF.Sqrt, ["fs0"], ["fsd"], scale=1.0 / D, bias=EPS)
                    B.recip(st_[:, 6:7], st_[:, 5:6], ["fsd"], ["frs"])
                    B.ts("pool", yb, yb, st_[:, 6:7], None, ALU.mult, None, yk + ["frs"], yk)
                    B.tt("pool", yb, yb, fing_bc, ALU.mult, yk + ["fing"], yk)
                    B.dma("sp", y_out[tt_ * 128:(tt_ + 1) * 128, :], yb, yk, [("Y", tt_)])
        self.barrier()
        A.release(mark)


_W_NAMES = ("w_ada", "b_ada", "norm1_g", "norm2_g", "w_in", "q_norm_g", "w_uq", "kv_norm_g", "w_ukv",
            "w_o_mla", "w_o_sb", "w_o_moba", "w_out", "w_ff1", "w_ff2", "final_norm_g")


def make_in_maps(inputs, declared=None):
    names = list(declared) if declared is not None else ["x", "c", "pos"] + list(_W_NAMES)
    src = {"x": ("x", np.float32), "c": ("c", np.float32), "pos": ("positions", np.int32)}
    shared = {}
    maps = [dict() for _ in range(8)]
    for n in names:
        if n in src:
            key, dt = src[n]
            arr = np.ascontiguousarray(np.asarray(inputs[key], dtype=dt))
            for b in range(8):
                maps[b][n] = arr[b]
        else:
            arr = np.ascontiguousarray(np.asarray(inputs[n], dtype=np.float32))
            for b in range(8):
                maps[b][n] = arr
    return maps


def kernel(**inputs):
    bld = Builder()
    nc = bld.build()
    res = run_bass_kernel_spmd(nc, make_in_maps(inputs, bld.declared), core_ids=list(range(8)))
    return np.stack([np.asarray(r["y"], dtype=np.float32) for r in res.results], axis=0)
```
